# Optimizing a Trainium2 kernel written in Bass

```python
import math
import jax, jax.numpy as jnp
from jax import lax
import numpy as np

D_MODEL = 2048
BATCH = 16
SEQ = 2048
DEPTH = 4

N_MEM = 256
HEAD_DIM = 128
DIFF_WIDTH = D_MODEL // 2
N_DIFF_HEADS = DIFF_WIDTH // HEAD_DIM
DIFF_QK_DIM = HEAD_DIM // 2
DIFF_QK_WIDTH = N_DIFF_HEADS * 2 * DIFF_QK_DIM
Q_BLOCK = 128
SGU_WIDTH = D_MODEL // 2
SGU_CHUNK = 128
SGU_GROUPS = SGU_WIDTH // HEAD_DIM
SGU_GROUP_DIM = SGU_WIDTH // SGU_GROUPS
CONV_WIDTH = D_MODEL // 2
CONV_TAPS = 3
N_BRANCHES = 3
MIX_SPLIT_SIZES = (DIFF_QK_WIDTH, DIFF_QK_WIDTH, DIFF_WIDTH,
                   SGU_WIDTH, SGU_WIDTH,
                   CONV_WIDTH, CONV_WIDTH, CONV_WIDTH,
                   N_BRANCHES * D_MODEL)
MIX_IN_WIDTH = sum(MIX_SPLIT_SIZES)
N_MEM_HEADS = 4
MEM_HEAD_DIM = 128
MEM_WIDTH = N_MEM_HEADS * MEM_HEAD_DIM
D_FF = ((8 * D_MODEL // 3 + 255) // 256) * 256
RMS_EPS = 1e-6

kernel_name = 'hybrid_diffattn_sgu_shortconv_macaron'


def rms_norm(x, g, eps=RMS_EPS):
    xf = x.astype(jnp.float32)
    y = xf * lax.rsqrt(jnp.mean(xf * xf, axis=-1, keepdims=True) + eps)
    return (y * g.astype(jnp.float32)).astype(x.dtype)


def swiglu(n, w_in, w_out):
    a, b = jnp.split(n @ w_in, 2, axis=-1)
    return (jax.nn.silu(a) * b) @ w_out


def alibi_slopes(n_heads):
    return 2.0 ** (-8.0 * jnp.arange(1, n_heads + 1, dtype=jnp.float32) / n_heads)


def diff_attention(q, k, v, lam, slopes):
    bsz, seq, n_heads, _, dq = q.shape
    n_blocks = seq // Q_BLOCK
    q = q * (1.0 / math.sqrt(dq))
    q_blocks = q.reshape(bsz, n_blocks, Q_BLOCK, n_heads, 2, dq).transpose(1, 0, 3, 4, 2, 5)
    k_t = k.transpose(0, 2, 3, 1, 4)
    v_t = v.transpose(0, 2, 1, 3)
    k_pos = jnp.arange(seq)

    def block(args):
        q_blk, start = args
        q_pos = start + jnp.arange(Q_BLOCK)
        dist = q_pos[:, None] - k_pos[None, :]
        bias = -slopes[:, None, None] * dist.astype(jnp.float32)
        s = jnp.einsum('bhiqd,bhikd->bhiqk', q_blk, k_t,
                       preferred_element_type=jnp.float32)
        s = jnp.where(dist >= 0, s + bias[:, None], -jnp.inf)
        p = jax.nn.softmax(s, axis=-1)
        w = p[:, :, 0] - lam * p[:, :, 1]
        return jnp.einsum('bhqk,bhkd->bhqd', w.astype(v_t.dtype), v_t)

    starts = jnp.arange(n_blocks) * Q_BLOCK
    o = lax.map(block, (q_blocks, starts))
    return o.transpose(1, 0, 3, 2, 4).reshape(bsz, seq, n_heads, v.shape[-1])


def hybrid_mixer(n, lam, lam_init, w_in, diff_subln, diff_w_out, sgu_norm, sgu_w_s, sgu_b,
                 sgu_w_out, conv_w, conv_w_out, w_o):
    bsz, seq, _ = n.shape
    offsets = np.cumsum(MIX_SPLIT_SIZES)[:-1].tolist()
    q, k, v, u, g_v, c_b, c_c, c_x, gate_logits = jnp.split(n @ w_in, offsets, axis=-1)

    q = q.reshape(bsz, seq, N_DIFF_HEADS, 2, DIFF_QK_DIM)
    k = k.reshape(bsz, seq, N_DIFF_HEADS, 2, DIFF_QK_DIM)
    v = v.reshape(bsz, seq, N_DIFF_HEADS, HEAD_DIM)
    o = diff_attention(q, k, v, lam, alibi_slopes(N_DIFF_HEADS))
    y_a = (rms_norm(o, diff_subln) * (1.0 - lam_init)).reshape(bsz, seq, DIFF_WIDTH)

    u = jax.nn.gelu(u, approximate=False)
    g_v = rms_norm(jax.nn.gelu(g_v, approximate=False), sgu_norm)
    g_v = g_v.reshape(bsz, seq // SGU_CHUNK, SGU_CHUNK, SGU_GROUPS, SGU_GROUP_DIM)
    w_s = jnp.tril(sgu_w_s)
    mixed = jnp.einsum('gts,bcsgd->bctgd', w_s, g_v) + sgu_b.T[None, None, :, :, None]
    y_b = u * mixed.reshape(bsz, seq, SGU_WIDTH)

    z = c_c * c_x
    zp = jnp.pad(z, ((0, 0), (CONV_TAPS - 1, 0), (0, 0)))
    conv = conv_w[0] * zp[:, 0:seq] + conv_w[1] * zp[:, 1:seq + 1] + conv_w[2] * zp[:, 2:seq + 2]
    y_c = c_b * conv

    gates = jax.nn.sigmoid(gate_logits.astype(jnp.float32)).astype(n.dtype)
    gates = gates.reshape(bsz, seq, N_BRANCHES, D_MODEL)
    merged = (gates[:, :, 0] * (y_a @ diff_w_out)
              + gates[:, :, 1] * (y_b @ sgu_w_out)
              + gates[:, :, 2] * (y_c @ conv_w_out))
    return merged @ w_o


def memory_attention(n, mem_n, w_q, w_kv, w_o):
    bsz, seq, _ = n.shape
    q = (n @ w_q).reshape(bsz, seq, N_MEM_HEADS, MEM_HEAD_DIM)
    kv = (mem_n @ w_kv).reshape(bsz, mem_n.shape[1], 2, N_MEM_HEADS, MEM_HEAD_DIM)
    k, v = kv[:, :, 0], kv[:, :, 1]
    s = jnp.einsum('bqhd,bkhd->bhqk', q, k, preferred_element_type=jnp.float32)
    p = jax.nn.softmax(s * (1.0 / math.sqrt(MEM_HEAD_DIM)), axis=-1)
    o = jnp.einsum('bhqk,bkhd->bqhd', p.astype(v.dtype), v).reshape(bsz, seq, MEM_WIDTH)
    return o @ w_o


def setup_inputs(seed: int = 0) -> dict:
    key = jax.random.key(seed)
    ks = iter(jax.random.split(key, 32))
    L = DEPTH

    def dense(shape, fan_in):
        return jax.random.normal(next(ks), shape, jnp.float32) * (fan_in ** -0.5)

    def gain(shape):
        return 1.0 + 0.02 * jax.random.normal(next(ks), shape, jnp.float32)

    return {
        'x': jax.random.normal(next(ks), (BATCH, SEQ, D_MODEL), jnp.float32),
        'mem': jax.random.normal(next(ks), (BATCH, N_MEM, D_MODEL), jnp.float32),
        'ffn1_norm': gain((L, D_MODEL)),
        'ffn1_w_in': dense((L, D_MODEL, 2 * D_FF), D_MODEL),
        'ffn1_w_out': dense((L, D_FF, D_MODEL), D_FF),
        'mix_norm': gain((L, D_MODEL)),
        'mix_w_in': dense((L, D_MODEL, MIX_IN_WIDTH), D_MODEL),
        'diff_lambda': 0.1 * jax.random.normal(next(ks), (L, 4, DIFF_QK_DIM), jnp.float32),
        'diff_subln': gain((L, HEAD_DIM)),
        'diff_w_out': dense((L, DIFF_WIDTH, D_MODEL), DIFF_WIDTH),
        'sgu_norm': gain((L, SGU_WIDTH)),
        'sgu_w_s': dense((L, SGU_GROUPS, SGU_CHUNK, SGU_CHUNK), SGU_CHUNK),
        'sgu_b': gain((L, SGU_GROUPS, SGU_CHUNK)),
        'sgu_w_out': dense((L, SGU_WIDTH, D_MODEL), SGU_WIDTH),
        'conv_w': dense((L, CONV_TAPS, CONV_WIDTH), CONV_TAPS),
        'conv_w_out': dense((L, CONV_WIDTH, D_MODEL), CONV_WIDTH),
        'mix_w_o': dense((L, D_MODEL, D_MODEL), D_MODEL),
        'xattn_norm': gain((L, D_MODEL)),
        'mem_norm': gain((L, D_MODEL)),
        'xattn_w_q': dense((L, D_MODEL, MEM_WIDTH), D_MODEL),
        'xattn_w_kv': dense((L, D_MODEL, 2 * MEM_WIDTH), D_MODEL),
        'xattn_w_o': dense((L, MEM_WIDTH, D_MODEL), MEM_WIDTH),
        'ffn2_norm': gain((L, D_MODEL)),
        'ffn2_w_in': dense((L, D_MODEL, 2 * D_FF), D_MODEL),
        'ffn2_w_out': dense((L, D_FF, D_MODEL), D_FF),
        'final_norm': gain((D_MODEL,)),
    }


def reference(x, mem, ffn1_norm, ffn1_w_in, ffn1_w_out, mix_norm, mix_w_in, diff_lambda,
              diff_subln, diff_w_out, sgu_norm, sgu_w_s, sgu_b, sgu_w_out, conv_w, conv_w_out,
              mix_w_o, xattn_norm, mem_norm, xattn_w_q, xattn_w_kv, xattn_w_o, ffn2_norm,
              ffn2_w_in, ffn2_w_out, final_norm):
    h = x
    for l in range(DEPTH):
        lam_init = 0.8 - 0.6 * math.exp(-0.3 * l)
        lp = diff_lambda[l].astype(jnp.float32)
        lam = jnp.exp(jnp.sum(lp[0] * lp[1])) - jnp.exp(jnp.sum(lp[2] * lp[3])) + lam_init

        h = h + 0.5 * swiglu(rms_norm(h, ffn1_norm[l]), ffn1_w_in[l], ffn1_w_out[l])
        h = h + hybrid_mixer(rms_norm(h, mix_norm[l]), lam, lam_init, mix_w_in[l], diff_subln[l],
                             diff_w_out[l], sgu_norm[l], sgu_w_s[l], sgu_b[l], sgu_w_out[l],
                             conv_w[l], conv_w_out[l], mix_w_o[l])
        h = h + memory_attention(rms_norm(h, xattn_norm[l]), rms_norm(mem, mem_norm[l]),
                                 xattn_w_q[l], xattn_w_kv[l], xattn_w_o[l])
        h = h + 0.5 * swiglu(rms_norm(h, ffn2_norm[l]), ffn2_w_in[l], ffn2_w_out[l])
    return rms_norm(h, final_norm)
```

```python
import math
from contextlib import ExitStack
import numpy as np
import ml_dtypes
import concourse.bass as bass
import concourse.mybir as mybir
from concourse.bass_utils import run_bass_kernel_spmd

F32 = mybir.dt.float32
BF16 = mybir.dt.bfloat16
AF = mybir.ActivationFunctionType
ALU = mybir.AluOpType

D = 2048
S = 2048
DEPTH = 4
NMEM = 256
DFF = 5632
MIXW = 14336
TT = 512
NCH = D // 128
EPS = 1e-6
NDS = 48
NSLOT = 4
SLOTE = 4096
NTMP = 5
NPT = 4
ENG = ("pe", "act", "dve", "pool", "sp")
SUBS = "fmxg"
DEBUG = False
NDBG = 24
P_GN, P_FN, P_CW, P_SUB, P_EPS, P_TOT = 0, 320, 336, 432, 436, 437


class Tok:
    __slots__ = ("w", "rc", "rd")

    def __init__(self):
        self.w = None
        self.rc = {}
        self.rd = []


class Op:
    __slots__ = ("eng", "fn", "deps", "sig", "sidx", "dma", "dsem", "dval")


class Sched:
    def __init__(self):
        self.ops = {e: [] for e in ENG}
        self.ndma = 0
        self.dma_ops = []

    def op(self, eng, fn, reads=(), writes=(), dma=False, extra=(), force=()):
        o = Op()
        o.eng = eng
        o.fn = fn
        o.sig = False
        o.dma = dma
        o.sidx = 0
        deps = list(extra) + list(force)
        for t in reads:
            if t.w is not None:
                deps.append(t.w)
        for t in writes:
            if t.w is not None:
                deps.append(t.w)
            deps.extend(t.rc.values())
            deps.extend(t.rd)
        if dma:
            n = self.ndma
            self.ndma += 1
            o.dsem = n % NDS
            o.dval = 16 * (n // NDS + 1)
            if n >= NDS:
                deps.append(self.dma_ops[n - NDS])
            self.dma_ops.append(o)
        dd = []
        seen = set()
        for d in deps:
            if d is o or id(d) in seen:
                continue
            seen.add(id(d))
            if (not d.dma) and d.eng == eng and not any(d is f for f in force):
                continue
            if not d.dma:
                d.sig = True
            dd.append(d)
        o.deps = dd
        for t in reads:
            if dma:
                t.rd.append(o)
            else:
                t.rc[eng] = o
        for t in writes:
            t.w = o
            t.rc = {}
            t.rd = []
        self.ops[eng].append(o)
        return o

    def prepare(self):
        for e in ENG:
            c = 0
            for o in self.ops[e]:
                if o.sig:
                    c += 1
                    o.sidx = c

    def emit_engine(self, e, eng, sems, dsems):
        waited = {}
        for o in self.ops[e]:
            for d in o.deps:
                if d.dma:
                    key, val, sem = ("d", d.dsem), d.dval, dsems[d.dsem]
                else:
                    key, val, sem = d.eng, d.sidx, sems[d.eng]
                if waited.get(key, 0) >= val:
                    continue
                waited[key] = val
                eng.wait_ge(sem, val)
            last = o.fn(eng) if o.fn is not None else None
            if o.dma:
                last.then_inc(dsems[o.dsem], 16)
            elif o.sig:
                last.then_inc(sems[e], 1)


def build_program(nseq=2, ntile=4, nlayer=DEPTH):
    nc = bass.Bass("TRN2", target_bir_lowering=False)
    L = nlayer
    dt = nc.dram_tensor
    xT = dt("xT", [nseq, D, S], F32, kind="ExternalInput")
    memT = dt("memT", [nseq, D, NMEM], F32, kind="ExternalInput")
    yT = dt("yT", [nseq, D, S], F32, kind="ExternalOutput")
    w_f1i = dt("ffn1_w_in", [DEPTH, D, 2 * DFF], F32, kind="ExternalInput")
    w_f1o = dt("ffn1_w_out", [DEPTH, DFF, D], F32, kind="ExternalInput")
    w_f2i = dt("ffn2_w_in", [DEPTH, D, 2 * DFF], F32, kind="ExternalInput")
    w_f2o = dt("ffn2_w_out", [DEPTH, DFF, D], F32, kind="ExternalInput")
    w_mix = dt("mix_w_in", [DEPTH, D, MIXW], F32, kind="ExternalInput")
    w_dout = dt("diff_w_out", [DEPTH, 1024, D], F32, kind="ExternalInput")
    w_sout = dt("sgu_w_out", [DEPTH, 1024, D], F32, kind="ExternalInput")
    w_cout = dt("conv_w_out", [DEPTH, 1024, D], F32, kind="ExternalInput")
    w_mo = dt("mix_w_o", [DEPTH, D, D], F32, kind="ExternalInput")
    w_xq = dt("xattn_w_q", [DEPTH, D, 512], F32, kind="ExternalInput")
    w_xkv = dt("xattn_w_kv", [DEPTH, D, 1024], F32, kind="ExternalInput")
    w_xo = dt("xattn_w_o", [DEPTH, 512, D], F32, kind="ExternalInput")
    d_lam = dt("diff_lambda", [DEPTH, 256], F32, kind="ExternalInput")
    d_sgn = dt("sgu_norm", [DEPTH, 1024], F32, kind="ExternalInput")
    d_sgb = dt("sgu_b", [DEPTH, 1024], F32, kind="ExternalInput")
    d_sws = dt("sgu_w_s", [DEPTH, 8, 128, 128], F32, kind="ExternalInput")
    d_par = dt("params", [128, P_TOT], F32, kind="ExternalInput")
    d_cbf = dt("cbf", [128, 5, 128], BF16, kind="ExternalInput")
    d_cf = dt("cf32", [128, 2, 128], F32, kind="ExternalInput")
    d_kaug = dt("kaug", [4, S], BF16, kind="ExternalInput")
    d_qaug = dt("qaug", [4, 8, S], BF16, kind="ExternalInput")
    kc = dt("kc", [DEPTH, nseq, 8, 128, S], BF16, kind="Internal")
    vc = dt("vc", [DEPTH, nseq, 8, S, 128], BF16, kind="Internal")
    mkd = dt("mkd", [DEPTH, nseq, 128, 4 * NMEM], BF16, kind="Internal")
    mvd = dt("mvd", [DEPTH, nseq, 128, 2 * 512], BF16, kind="Internal")
    NBLK = 232
    wscs = [dt("wsc%d" % l_, [NBLK, 128, SLOTE], BF16, kind="Internal") for l_ in range(DEPTH)]

    if DEBUG:
        dbgf = dt("dbgf", [NDBG, 128, 516], F32, kind="ExternalOutput")
        dbgb = dt("dbgb", [NDBG, 128, 512], BF16, kind="ExternalOutput")
    dbgn = {"f": 0, "b": 0}
    SC = Sched()
    es = ExitStack()
    with es:
        def sb(name, shape, dtype):
            return es.enter_context(nc.sbuf_tensor(name, shape, dtype))

        H = sb("H", [128, NCH, TT], F32)
        XN = sb("XN", [128, NCH, TT], BF16)
        BIG = sb("BIG", [128, 44, TT], BF16)
        WR = sb("WR", [128, NSLOT, SLOTE], BF16)
        RB = sb("RB", [128, TT], F32)
        TMP = sb("TMP", [128, NTMP, 516], F32)
        ACC = sb("ACC", [128, 2, TT], F32)
        PT = sb("PT", [128, NPT, TT], BF16)
        KH = sb("KH", [128, 2, S], BF16)
        VH = sb("VH", [128, 2, 16, 128], BF16)
        QA = sb("QA", [128, 2, TT], BF16)
        KAUG = sb("KAUG", [128, S], BF16)
        MK = sb("MK", [128, 4, NMEM], BF16)
        MV = sb("MV", [128, 2, 512], BF16)
        GB = sb("GB", [128, 2, 1024], F32)
        WSR = sb("WSR", [128, 8, 128], F32)
        WS = sb("WS", [128, 8, 128], BF16)
        PAR = sb("PAR", [128, P_TOT], F32)
        CBF = sb("CBF", [128, 5, 128], BF16)
        CF = sb("CF", [128, 2, 128], F32)
        NLAM = sb("NLAM", [128, DEPTH], F32)
        SUBL = sb("SUBL", [128, DEPTH], F32)
        CARRY = sb("CARRY", [128, DEPTH, 8, 2], F32)
        SS = sb("SS", [128, 16], F32)
        PS = es.enter_context(nc.psum_tensor("PS", [128, 8, 512], F32))
        sems = {e: es.enter_context(nc.semaphore("s_" + e)) for e in ENG}
        dsems = [es.enter_context(nc.semaphore("d%d" % i)) for i in range(NDS)]

        IDB = CBF[:, 0, :]
        ONE_D = CBF[:, 1, :]
        ONE_H = CBF[:, 2, :]
        ONE_1 = CBF[:, 3, :]
        CMASK = CBF[:, 4, :]
        IDF = CF[:, 0, :]
        TRIL = CF[:, 1, :]

        tH = [Tok() for _ in range(NCH)]
        tXN = [Tok() for _ in range(NCH)]
        tBIG = [Tok() for _ in range(44)]
        tWR = [Tok() for _ in range(NSLOT)]
        tRB = Tok()
        tTMP = [Tok() for _ in range(NTMP)]
        tPT = [Tok() for _ in range(NPT)]
        tKH = [Tok(), Tok()]
        tVH = [Tok(), Tok()]
        tQA = [Tok(), Tok()]
        tQA2 = [Tok(), Tok()]
        tMK, tMV, tGB, tWSR, tWS = Tok(), Tok(), Tok(), Tok(), Tok()
        tACC = [Tok(), Tok()]
        tCARRY = [[Tok() for _ in range(8)] for _ in range(DEPTH)]
        tSS = Tok()
        tPS = [Tok() for _ in range(8)]
        tKC = {}
        tVC = {}
        tMKD = {}
        tMVD = {}
        tMISC = Tok()
        tMISC2 = Tok()

        cnt = {"b": 0, "t": 0, "p": 0, "w": 0, "z": 0, "h": 0}

        def nb():
            cnt["b"] = (cnt["b"] + 1) % 8
            return cnt["b"]

        def nt():
            cnt["t"] = (cnt["t"] + 1) % NTMP
            return cnt["t"]

        def npt():
            cnt["p"] = (cnt["p"] + 1) % NPT
            return cnt["p"]

        def gn(l, n, c):
            i = P_GN + l * 80 + n * 16 + c
            return PAR[:, i:i + 1]

        out_dmas = []
        def dumpf(name, ap, n, reads):
            if not DEBUG or dbgn["f"] >= NDBG:
                return
            i = dbgn["f"]
            dbgn["f"] += 1
            print("DUMPF", i, name)
            out_dmas.append(dma("sp", dbgf.ap()[i, :, 0:n], ap, reads=reads))

        def dumpb(name, ap, n, reads):
            if not DEBUG or dbgn["b"] >= NDBG:
                return
            i = dbgn["b"]
            dbgn["b"] += 1
            print("DUMPB", i, name)
            out_dmas.append(dma("sp", dbgb.ap()[i, :, 0:n], ap, reads=reads))

        def dma(eng, out, in_, reads=(), writes=()):
            return SC.op(eng, lambda e: e.dma_start(out=out, in_=in_), reads=reads, writes=writes, dma=True)

        wmode = {"m": "direct", "blk": 0}
        tWSC = {}

        def wload(src, kcn, ncols):
            s = cnt["w"]
            cnt["w"] = (s + 1) % NSLOT
            E = kcn * ncols
            dst = WR[:, s, 0:E].rearrange("p (k n) -> p k n", k=kcn)
            if wmode["m"] == "direct":
                dma("pool", dst, src, writes=[tWR[s]])
                return s, dst
            blk = wmode["blk"]
            wmode["blk"] += 1
            wsc = wscs[wmode["l"]]
            blk_key = (wmode["l"], blk)
            if wmode["m"] == "fill":
                dma("pool", dst, src, writes=[tWR[s]])
                tWSC[blk_key] = Tok()
                dma("sp", wsc.ap()[blk, :, 0:E], WR[:, s, 0:E], reads=[tWR[s]], writes=[tWSC[blk_key]])
            else:
                dma("pool", WR[:, s, 0:E], wsc.ap()[blk, :, 0:E], reads=[tWSC[blk_key]], writes=[tWR[s]])
            return s, dst

        fresh = {"n": 0}

        def mm(out, pairs, reads, wtok, first=True, last=True):
            n = len(pairs)
            if (fresh["n"] > 0 and n == NCH and first and last and len(reads) >= NCH
                    and all(a is b for a, b in zip(reads[-NCH:], tXN))):
                fresh["n"] -= 1
                base = list(reads[:-NCH])
                o = None
                for k, (lt, rh) in enumerate(pairs):
                    o = SC.op("pe", lambda pe, lt=lt, rh=rh, k=k: pe.matmul(out, lt, rh, start=(k == 0), stop=(k == n - 1)),
                              reads=base + [tXN[k]], writes=[wtok])
                return o

            def fn(pe):
                ins = None
                for i, (lt, rh) in enumerate(pairs):
                    ins = pe.matmul(out, lt, rh, start=(first and i == 0), stop=(last and i == n - 1))
                return ins
            return SC.op("pe", fn, reads=reads, writes=[wtok])

        def act(out, in_, func, reads, writes, scale=1.0):
            return SC.op("act", lambda e: e.activation(out, in_, func, scale=scale), reads=reads, writes=writes)

        def dve(fn, reads, writes):
            return SC.op("dve", fn, reads=reads, writes=writes)

        def tt(out, a, b, op, reads, writes):
            return dve(lambda e: e.tensor_tensor(out, a, b, op), reads, writes)

        def rsqrt(out, in_, reads, writes, scale=1.0):
            SC.op("act", lambda e: e.activation(out, in_, AF.Sqrt, bias=PAR[:, P_EPS:P_EPS + 1], scale=scale),
                  reads=reads, writes=writes)
            return dve(lambda e: e.reciprocal(out, out), writes, writes)

        def rmsnorm(gain, n=TT, final=False):
            for c in range(NCH):
                act(XN[:, c, 0:n], H[:, c, 0:n], AF.Square, [tH[c]], [tXN[c]])
            b = nb()
            mm(PS[:, b, 0:n], [(ONE_D, XN[:, c, 0:n]) for c in range(NCH)], tXN, tPS[b])
            rsqrt(RB[:, 0:n], PS[:, b, 0:n], [tPS[b]], [tRB])
            if gain is None:
                return
            if not final:
                fresh["n"] = 2
            for c in range(NCH):
                o = H[:, c, 0:n] if final else XN[:, c, 0:n]
                dve(lambda e, o=o, c=c: e.scalar_tensor_tensor(o, H[:, c, 0:n], gain(c), RB[:, 0:n],
                                                               op0=ALU.mult, op1=ALU.mult),
                    [tH[c], tRB], [tH[c]] if final else [tXN[c]])

        def proj_fm(wv, col0, nchunks, kcn, rhs_fn, rhs_toks, evac, n=TT):
            j = 0
            while j < nchunks:
                nsub = min(2, nchunks - j)
                s, wv_s = wload(wv[:, 0:kcn, col0 + j * 128: col0 + (j + nsub) * 128], kcn, nsub * 128)
                for sub in range(nsub):
                    b = nb()
                    mm(PS[:, b, 0:n], [(wv_s[:, k, sub * 128:(sub + 1) * 128], rhs_fn(k)) for k in range(kcn)],
                       [tWR[s]] + rhs_toks, tPS[b])
                    evac(j + sub, b)
                j += nsub

        def proj_tm(wv, col0, ncb, lhs_fn, lhs_toks, ntc, evac):
            for cb in range(ncb):
                sl = []
                for kh in range(2):
                    sl.append(wload(wv[:, kh * 8:(kh + 1) * 8, col0 + cb * 512: col0 + (cb + 1) * 512], 8, 512))
                for tc in range(ntc):
                    b = nb()
                    mm(PS[:, b, :], [(lhs_fn(k, tc), sl[k // 8][1][:, k % 8, :]) for k in range(NCH)],
                       [tWR[sl[0][0]], tWR[sl[1][0]]] + lhs_toks, tPS[b])
                    evac(tc, cb, b)

        def ffn(l, w_in, w_out, nidx):
            wi = w_in.ap()[l].rearrange("(k p) n -> p k n", p=128)
            wo = w_out.ap()[l].rearrange("(k p) n -> p k n", p=128)
            rmsnorm(lambda c: gn(l, nidx, c))
            for jj in range(22):
                sa, va = wload(wi[:, :, jj * 256:(jj + 1) * 256], 16, 256)
                sb_, vb = wload(wi[:, :, DFF + jj * 256: DFF + (jj + 1) * 256], 16, 256)
                for sub in range(2):
                    j = 2 * jj + sub
                    ba = nb()
                    mm(PS[:, ba, :], [(va[:, k, sub * 128:(sub + 1) * 128], XN[:, k, :]) for k in range(NCH)],
                       [tWR[sa]] + tXN, tPS[ba])
                    bb = nb()
                    mm(PS[:, bb, :], [(vb[:, k, sub * 128:(sub + 1) * 128], XN[:, k, :]) for k in range(NCH)],
                       [tWR[sb_]] + tXN, tPS[bb])
                    t = nt()
                    act(TMP[:, t, 0:TT], PS[:, ba, :], AF.Silu, [tPS[ba]], [tTMP[t]])
                    tt(BIG[:, j, :], TMP[:, t, 0:TT], PS[:, bb, :], ALU.mult, [tTMP[t], tPS[bb]], [tBIG[j]])
            for qd in range(4):
                banks = [nb() for _ in range(4)]
                for ks in range(6):
                    k0 = ks * 8
                    kn = min(8, 44 - k0)
                    s, wv_s = wload(wo[:, k0:k0 + kn, qd * 512:(qd + 1) * 512], kn, 512)
                    for mi in range(4):
                        mm(PS[:, banks[mi], :],
                           [(wv_s[:, kk, mi * 128:(mi + 1) * 128], BIG[:, k0 + kk, :]) for kk in range(kn)],
                           [tWR[s]] + tBIG[k0:k0 + kn], tPS[banks[mi]], first=(ks == 0), last=(ks == 5))
                for mi in range(4):
                    c = qd * 4 + mi
                    b = banks[mi]
                    dve(lambda e, c=c, b=b: e.scalar_tensor_tensor(H[:, c, :], PS[:, b, :], 0.5, H[:, c, :],
                                                                   op0=ALU.mult, op1=ALU.add),
                        [tPS[b], tH[c]], [tH[c]])

        def mixer(l, sq, ti):
            t0 = ti * TT
            wv = w_mix.ap()[l].rearrange("(k p) n -> p k n", p=128)
            rmsnorm(lambda c: gn(l, 1, c))
            xn_rhs = lambda k: XN[:, k, :]
            dma("sp", GB[:, 0, :], bass.AP(d_sgn, l * 1024, [[0, 128], [1, 1024]]), writes=[tGB])
            tGB2 = tMISC2
            dma("sp", GB[:, 1, :], bass.AP(d_sgb, l * 1024, [[0, 128], [1, 1024]]), writes=[tGB2])
            gb_ops = list(SC.dma_ops[-2:])
            dma("sp", WSR[:, :, :], d_sws.ap()[l].rearrange("g t s -> t g s"), writes=[tWSR])

            def ev_k(h, b):
                act(BIG[:, 24 + h, :], PS[:, b, :], AF.Copy, [tPS[b]], [tBIG[24 + h]])
                tk = Tok()
                tKC[(l, sq, ti, h)] = tk
                dma("sp", kc.ap()[l, sq, h, :, t0:t0 + TT], BIG[:, 24 + h, :], reads=[tBIG[24 + h]], writes=[tk])
            proj_fm(wv, 1024, 8, 16, xn_rhs, tXN, ev_k)

            def ev_v(tc, cb, b):
                ci = 8 + tc * 2 + cb
                act(BIG[:, ci, :], PS[:, b, :], AF.Copy, [tPS[b]], [tBIG[ci]])
            proj_tm(wv, 2048, 2, lambda k, tc: XN[:, k, tc * 128:(tc + 1) * 128], tXN, 4, ev_v)
            for tc in range(4):
                tk = Tok()
                tVC[(l, sq, ti, tc)] = tk
                src = BIG[:, 8 + 2 * tc: 10 + 2 * tc, :].rearrange("p a (h d) -> p (a h) d", d=128)
                dst = vc.ap()[l, sq, :, t0 + tc * 128: t0 + (tc + 1) * 128, :].rearrange("h p d -> p h d")
                dma("sp", dst, src, reads=[tBIG[8 + 2 * tc], tBIG[9 + 2 * tc]], writes=[tk])

            def ev_q(h, b):
                act(BIG[:, h, :], PS[:, b, :], AF.Copy, [tPS[b]], [tBIG[h]], scale=0.125)
            proj_fm(wv, 0, 8, 16, xn_rhs, tXN, ev_q)

            def ev_u(c, b):
                act(BIG[:, 16 + c, :], PS[:, b, :], AF.Gelu, [tPS[b]], [tBIG[16 + c]])
            proj_fm(wv, 3072, 8, 16, xn_rhs, tXN, ev_u)

            def ev_gv(tc, cb, b):
                ci = 24 + tc * 2 + cb
                act(BIG[:, ci, :], PS[:, b, :], AF.Gelu, [tPS[b]], [tBIG[ci]])
                t = nt()
                col = tc * 2 + cb
                dve(lambda e: e.scalar_tensor_tensor(TMP[:, t, 0:TT], BIG[:, ci, :], 1.0, BIG[:, ci, :],
                                                     op0=ALU.mult, op1=ALU.mult, accum_out=SS[:, col:col + 1]),
                    [tBIG[ci]], [tTMP[t], tSS])
            proj_tm(wv, 4096, 2, lambda k, tc: XN[:, k, tc * 128:(tc + 1) * 128], tXN, 4, ev_gv)
            sep = dve(lambda e: e.tensor_copy(SS[:, 15:16], SS[:, 14:15]), [tSS], [tSS])
            for tc in range(4):
                SC.op("dve", lambda e, tc=tc: e.tensor_tensor(SS[:, 8 + tc:9 + tc], SS[:, 2 * tc:2 * tc + 1],
                                                              SS[:, 2 * tc + 1:2 * tc + 2], ALU.add),
                      reads=[tSS], writes=[tSS], force=[sep])
                rop = rsqrt(SS[:, 12 + tc:13 + tc], SS[:, 8 + tc:9 + tc], [tSS], [tSS], scale=1.0 / 1024)
                for cb in range(2):
                    ci = 24 + tc * 2 + cb
                    SC.op("dve", lambda e, tc=tc, cb=cb, ci=ci: e.scalar_tensor_tensor(
                        BIG[:, ci, :], BIG[:, ci, :], SS[:, 12 + tc:13 + tc], GB[:, 0, cb * 512:(cb + 1) * 512],
                        op0=ALU.mult, op1=ALU.mult), reads=[tBIG[ci], tSS, tGB], writes=[tBIG[ci]], force=[rop])

            for cp in range(4):
                s3 = [wload(wv[:, :, off + cp * 256: off + (cp + 1) * 256], 16, 256) for off in (5120, 6144, 7168)]
                for sub in range(2):
                    c = 2 * cp + sub
                    bs = []
                    for (s, v) in s3:
                        b = nb()
                        mm(PS[:, b, :], [(v[:, k, sub * 128:(sub + 1) * 128], XN[:, k, :]) for k in range(NCH)],
                           [tWR[s]] + tXN, tPS[b])
                        bs.append(b)
                    b_b, b_c, b_x = bs
                    t1 = nt()
                    act(TMP[:, t1, 0:TT], PS[:, b_x, :], AF.Copy, [tPS[b_x]], [tTMP[t1]])
                    z = nt()
                    tt(TMP[:, z, 2:514], PS[:, b_c, :], TMP[:, t1, 0:TT], ALU.mult, [tPS[b_c], tTMP[t1]], [tTMP[z]])
                    cop = dve(lambda e, z=z, c=c: e.tensor_copy(TMP[:, z, 0:2], CARRY[:, l, c, :]), [tCARRY[l][c]], [tTMP[z]])
                    dve(lambda e, z=z, c=c: e.tensor_copy(CARRY[:, l, c, :], TMP[:, z, 512:514]), [tTMP[z]],
                        [tCARRY[l][c]])
                    t2 = nt()
                    cw = lambda tap, c=c: PAR[:, P_CW + l * 24 + tap * 8 + c: P_CW + l * 24 + tap * 8 + c + 1]
                    SC.op("dve", lambda e, z=z, t2=t2, cw=cw: e.tensor_scalar(TMP[:, t2, 0:TT], TMP[:, z, 0:512], cw(0), None,
                                                                              op0=ALU.mult),
                          reads=[tTMP[z]], writes=[tTMP[t2]], force=[cop])
                    for tap in (1, 2):
                        dve(lambda e, z=z, t2=t2, cw=cw, tap=tap: e.scalar_tensor_tensor(
                            TMP[:, t2, 0:TT], TMP[:, z, tap:tap + 512], cw(tap), TMP[:, t2, 0:TT],
                            op0=ALU.mult, op1=ALU.add), [tTMP[z], tTMP[t2]], [tTMP[t2]])
                    if c < 4 and DEBUG:
                        dumpf("Z%d" % c, TMP[:, z, 0:514], 514, [tTMP[z]])
                        dumpf("T2_%d" % c, TMP[:, t2, 0:TT], 512, [tTMP[t2]])
                    tt(BIG[:, 32 + c, :], TMP[:, t2, 0:TT], PS[:, b_b, :], ALU.mult, [tTMP[t2], tPS[b_b]], [tBIG[32 + c]])
                    if DEBUG:
                        dumpb("YC%d" % c, BIG[:, 32 + c, :], 512, [tBIG[32 + c]])

            for half in range(2):
                b = nb()

                def fn(pe, half=half, b=b):
                    ins = None
                    for gi in range(4):
                        ins = pe.transpose(PS[:, b, gi * 128:(gi + 1) * 128], WSR[:, half * 4 + gi, :], IDF)
                    return ins
                SC.op("pe", fn, reads=[tWSR], writes=[tPS[b]])
                for gi in range(4):
                    g = half * 4 + gi
                    tt(WS[:, g, :], PS[:, b, gi * 128:(gi + 1) * 128], TRIL, ALU.mult, [tPS[b]], [tWS])

            for g in range(8):
                b = nb()

                def fn(pe, g=g, b=b):
                    ins = None
                    for pc in range(4):
                        ins = pe.matmul(PS[:, b, pc * 128:(pc + 1) * 128],
                                        BIG[:, 24 + pc * 2 + g // 4, (g % 4) * 128:(g % 4 + 1) * 128],
                                        WS[:, g, :], start=True, stop=True)
                    return ins
                SC.op("pe", fn, reads=[tWS] + tBIG[24:32], writes=[tPS[b]])
                t = nt()
                for pc in range(4):
                    tt(TMP[:, t, pc * 128:(pc + 1) * 128], PS[:, b, pc * 128:(pc + 1) * 128],
                       GB[:, 1, g * 128:(g + 1) * 128], ALU.add, [tPS[b], tGB2], [tTMP[t]])
                tt(BIG[:, 16 + g, :], TMP[:, t, 0:TT], BIG[:, 16 + g, :], ALU.mult, [tTMP[t], tBIG[16 + g]], [tBIG[16 + g]])

            dumpb("WS0", WS[:, 0, :], 128, [tWS])
            dumpb("YB16", BIG[:, 16, :], 512, [tBIG[16]])
            dumpf("GB1", GB[:, 1, 0:512], 512, [tGB2])
            nkb = (t0 + TT) // 128
            nk = t0 + TT
            pending = [None]
            for h in range(8):
                hs = cnt["h"]
                cnt["h"] = 1 - hs
                dma("sp", KH[:, hs, 0:nk], kc.ap()[l, sq, h, :, 0:nk],
                    reads=[tKC[(l, sq, tj, h)] for tj in range(ti + 1)], writes=[tKH[hs]])
                dma("sp", VH[:, hs, 0:nkb, :], vc.ap()[l, sq, h, 0:nk, :].rearrange("(b p) d -> p b d", p=128),
                    reads=[tVC[(l, sq, tj, tc)] for tj in range(ti + 1) for tc in range(4)], writes=[tVH[hs]])
                dma("sp", QA[0:4, hs, :], d_qaug.ap()[:, h, t0:t0 + TT], writes=[tQA[hs]])
                dma("sp", QA[64:68, hs, :], d_qaug.ap()[:, h, t0:t0 + TT], writes=[tQA2[hs]])
                bo = [0, 1]
                bz = [2, 3]
                unit_p = {}

                def emit_scores(kb, hs=hs, h=h, unit_p=unit_p):
                    r = kb - t0 // 128
                    n0 = 128 * r if r > 0 else 0
                    bs2 = []
                    for i in range(2):
                        cnt["sb"] = (cnt.get("sb", 0) + 1) % 4
                        bs2.append(4 + cnt["sb"])

                    def fn(pe, kb=kb, r=r, n0=n0, bs2=bs2, hs=hs, h=h):
                        for i in range(2):
                            pe.matmul(PS[:, bs2[i], n0:TT], KH[64 * i:64 * i + 64, hs, kb * 128:(kb + 1) * 128],
                                      BIG[64 * i:64 * i + 64, h, n0:TT], start=True, stop=False)
                        if r >= 0:
                            for i in range(2):
                                pe.matmul(PS[:, bs2[i], n0:n0 + 128], IDB, CMASK, start=False, stop=False)
                        ins = None
                        for i in range(2):
                            ins = pe.matmul(PS[:, bs2[i], n0:TT], KAUG[64 * i:64 * i + 4, kb * 128:(kb + 1) * 128],
                                            QA[64 * i:64 * i + 4, hs, n0:TT], start=False, stop=True)
                        return ins
                    SC.op("pe", fn, reads=[tKH[hs], tBIG[h], tQA[hs], tQA2[hs]], writes=[tPS[bs2[0]], tPS[bs2[1]]])
                    for i in range(2):
                        p = npt()
                        act(PT[:, p, n0:TT], PS[:, bs2[i], n0:TT], AF.Exp, [tPS[bs2[i]]], [tPT[p]])
                        unit_p[(kb, i)] = (p, n0)

                def emit_pv(kb, hs=hs, unit_p=unit_p, bo=bo, bz=bz):
                    for i in range(2):
                        p, n0 = unit_p[(kb, i)]

                        def fn2(pe, i=i, kb=kb, n0=n0, p=p, hs=hs, bo=bo, bz=bz):
                            pe.matmul(PS[:, bo[i], n0:TT], VH[:, hs, kb, :], PT[:, p, n0:TT],
                                      start=(kb == 0), stop=(kb == nkb - 1))
                            return pe.matmul(PS[:, bz[i], n0:TT], ONE_1, PT[:, p, n0:TT],
                                             start=(kb == 0), stop=(kb == nkb - 1))
                        SC.op("pe", fn2, reads=[tVH[hs], tPT[p]], writes=[tPS[bo[i]], tPS[bz[i]]])

                emit_scores(0)
                if pending[0] is not None:
                    pending[0]()
                    pending[0] = None
                for kb in range(nkb):
                    if kb + 1 < nkb:
                        emit_scores(kb + 1)
                    emit_pv(kb)
                ta, tb_, tc_ = nt(), nt(), nt()
                dve(lambda e, ta=ta, bz=bz: e.reciprocal(TMP[:, ta, 0:TT], PS[:, bz[0], :]), [tPS[bz[0]]], [tTMP[ta]])
                tt(TMP[:, tb_, 0:TT], PS[:, bo[0], :], TMP[:, ta, 0:TT], ALU.mult, [tPS[bo[0]], tTMP[ta]], [tTMP[tb_]])
                dve(lambda e, ta=ta, bz=bz: e.reciprocal(TMP[:, ta, 0:TT], PS[:, bz[1], :]), [tPS[bz[1]]], [tTMP[ta]])
                tt(TMP[:, tc_, 0:TT], PS[:, bo[1], :], TMP[:, ta, 0:TT], ALU.mult, [tPS[bo[1]], tTMP[ta]], [tTMP[tc_]])
                dve(lambda e, tb_=tb_, tc_=tc_: e.scalar_tensor_tensor(TMP[:, tb_, 0:TT], TMP[:, tc_, 0:TT], NLAM[:, l:l + 1],
                                                                       TMP[:, tb_, 0:TT], op0=ALU.mult, op1=ALU.add),
                    [tTMP[tb_], tTMP[tc_]], [tTMP[tb_]])
                p = npt()
                act(PT[:, p, :], TMP[:, tb_, 0:TT], AF.Square, [tTMP[tb_]], [tPT[p]])

                def fin(h=h, p=p, ta=ta, tb_=tb_):
                    cnt["sb"] = (cnt.get("sb", 0) + 1) % 4
                    b = 4 + cnt["sb"]
                    mm(PS[:, b, :], [(ONE_H, PT[:, p, :])], [tPT[p]], tPS[b])
                    rsqrt(TMP[:, ta, 0:TT], PS[:, b, :], [tPS[b]], [tTMP[ta]])
                    dve(lambda e: e.scalar_tensor_tensor(BIG[:, 8 + h, :], TMP[:, tb_, 0:TT], SUBL[:, l:l + 1],
                                                         TMP[:, ta, 0:TT], op0=ALU.mult, op1=ALU.mult),
                        [tTMP[ta], tTMP[tb_]], [tBIG[8 + h]])
                pending[0] = fin
            pending[0]()
            pending[0] = None

            bw = [w_dout.ap()[l].rearrange("(k p) n -> p k n", p=128),
                  w_sout.ap()[l].rearrange("(k p) n -> p k n", p=128),
                  w_cout.ap()[l].rearrange("(k p) n -> p k n", p=128)]
            ybase = [8, 16, 32]
            mch = lambda m: m if m < 8 else 24 + (m - 8)
            for mp in range(8):
                for br in range(3):
                    sg, vg = wload(wv[:, :, 8192 + br * 2048 + mp * 256: 8192 + br * 2048 + (mp + 1) * 256], 16, 256)
                    sw, vw = wload(bw[br][:, :, mp * 256:(mp + 1) * 256], 8, 256)
                    yb = ybase[br]
                    for sub in range(2):
                        m = 2 * mp + sub
                        b = nb()
                        mm(PS[:, b, :], [(vg[:, k, sub * 128:(sub + 1) * 128], XN[:, k, :]) for k in range(NCH)],
                           [tWR[sg]] + tXN, tPS[b])
                        t = nt()
                        act(TMP[:, t, 0:TT], PS[:, b, :], AF.Sigmoid, [tPS[b]], [tTMP[t]])
                        b2 = nb()
                        mm(PS[:, b2, :], [(vw[:, k, sub * 128:(sub + 1) * 128], BIG[:, yb + k, :]) for k in range(8)],
                           [tWR[sw]] + tBIG[yb:yb + 8], tPS[b2])
                        if br == 0:
                            tt(ACC[:, sub, :], TMP[:, t, 0:TT], PS[:, b2, :], ALU.mult, [tTMP[t], tPS[b2]], [tACC[sub]])
                        else:
                            tt(TMP[:, t, 0:TT], TMP[:, t, 0:TT], PS[:, b2, :], ALU.mult, [tTMP[t], tPS[b2]], [tTMP[t]])
                            if br == 1:
                                tt(ACC[:, sub, :], ACC[:, sub, :], TMP[:, t, 0:TT], ALU.add, [tACC[sub], tTMP[t]],
                                   [tACC[sub]])
                            else:
                                tt(BIG[:, mch(m), :], ACC[:, sub, :], TMP[:, t, 0:TT], ALU.add, [tACC[sub], tTMP[t]],
                                   [tBIG[mch(m)]])

            wo = w_mo.ap()[l].rearrange("(k p) n -> p k n", p=128)

            def ev_o(c, b):
                tt(H[:, c, :], PS[:, b, :], H[:, c, :], ALU.add, [tPS[b], tH[c]], [tH[c]])
            proj_fm(wo, 0, 16, 16, lambda k: BIG[:, mch(k), :], [tBIG[mch(k)] for k in range(16)], ev_o)

        def xattn(l, sq):
            rmsnorm(lambda c: gn(l, 2, c))
            dma("sp", MK[:, :, :], mkd.ap()[l, sq].rearrange("p (h m) -> p h m", h=4), reads=[tMKD[(l, sq)]],
                writes=[tMK])
            dma("sp", MV[:, :, :], mvd.ap()[l, sq].rearrange("p (b n) -> p b n", b=2), reads=[tMVD[(l, sq)]],
                writes=[tMV])
            wq = w_xq.ap()[l].rearrange("(k p) n -> p k n", p=128)

            def ev_q(h, b):
                act(BIG[:, h, :], PS[:, b, :], AF.Copy, [tPS[b]], [tBIG[h]])
            proj_fm(wq, 0, 4, 16, lambda k: XN[:, k, :], tXN, ev_q)
            sc = 1.0 / math.sqrt(128.0)
            for h in range(4):
                bo, bz = nb(), nb()
                for mb in range(2):
                    bsc = nb()
                    mm(PS[:, bsc, :], [(MK[:, h, mb * 128:(mb + 1) * 128], BIG[:, h, :])], [tMK, tBIG[h]], tPS[bsc])
                    p = npt()
                    act(PT[:, p, :], PS[:, bsc, :], AF.Exp, [tPS[bsc]], [tPT[p]], scale=sc)

                    def fn2(pe, h=h, mb=mb, p=p, bo=bo, bz=bz):
                        pe.matmul(PS[:, bo, :], MV[:, mb, h * 128:(h + 1) * 128], PT[:, p, :], start=(mb == 0),
                                  stop=(mb == 1))
                        return pe.matmul(PS[:, bz, :], ONE_1, PT[:, p, :], start=(mb == 0), stop=(mb == 1))
                    SC.op("pe", fn2, reads=[tMV, tPT[p]], writes=[tPS[bo], tPS[bz]])
                ta = nt()
                dve(lambda e, ta=ta, bz=bz: e.reciprocal(TMP[:, ta, 0:TT], PS[:, bz, :]), [tPS[bz]], [tTMP[ta]])
                tt(BIG[:, 8 + h, :], PS[:, bo, :], TMP[:, ta, 0:TT], ALU.mult, [tPS[bo], tTMP[ta]], [tBIG[8 + h]])
            wo = w_xo.ap()[l].rearrange("(k p) n -> p k n", p=128)
            for blk in range(4):
                s, v = wload(wo[:, :, blk * 512:(blk + 1) * 512], 4, 512)
                for sub in range(4):
                    c = blk * 4 + sub
                    b = nb()
                    mm(PS[:, b, :], [(v[:, k, sub * 128:(sub + 1) * 128], BIG[:, 8 + k, :]) for k in range(4)],
                       [tWR[s]] + tBIG[8:12], tPS[b])
                    tt(H[:, c, :], PS[:, b, :], H[:, c, :], ALU.add, [tPS[b], tH[c]], [tH[c]])

        def mem_prep(sq):
            n = NMEM
            dma("sp", H[:, :, 0:n], memT.ap()[sq].rearrange("(k p) n -> p k n", p=128), writes=tH)
            rmsnorm(None, n=n)
            for l in range(L):
                for c in range(NCH):
                    dve(lambda e, c=c, l=l: e.scalar_tensor_tensor(XN[:, c, 0:n], H[:, c, 0:n], gn(l, 3, c), RB[:, 0:n],
                                                                   op0=ALU.mult, op1=ALU.mult), [tH[c], tRB], [tXN[c]])
                wkv = w_xkv.ap()[l].rearrange("(k p) n -> p k n", p=128)

                def ev_k(h, b):
                    act(MK[:, h, :], PS[:, b, 0:n], AF.Copy, [tPS[b]], [tMK])
                proj_fm(wkv, 0, 4, 16, lambda k: XN[:, k, 0:n], tXN, ev_k, n=n)

                def ev_v(tc, cb, b):
                    act(MV[:, tc, :], PS[:, b, :], AF.Copy, [tPS[b]], [tMV])
                proj_tm(wkv, 512, 1, lambda k, tc: XN[:, k, tc * 128:(tc + 1) * 128], tXN, 2, ev_v)
                tMKD[(l, sq)] = Tok()
                tMVD[(l, sq)] = Tok()
                dma("sp", mkd.ap()[l, sq].rearrange("p (h m) -> p h m", h=4), MK[:, :, :], reads=[tMK],
                    writes=[tMKD[(l, sq)]])
                dma("sp", mvd.ap()[l, sq].rearrange("p (b n) -> p b n", b=2), MV[:, :, :], reads=[tMV],
                    writes=[tMVD[(l, sq)]])

        cl = [dma("sp", PAR[:, :], d_par.ap(), writes=[tMISC]),
              dma("sp", CBF[:, :, :], d_cbf.ap(), writes=[tMISC]),
              dma("sp", CF[:, :, :], d_cf.ap(), writes=[tMISC]),
              dma("sp", KAUG[0:4, :], d_kaug.ap(), writes=[tMISC]),
              dma("sp", KAUG[64:68, :], d_kaug.ap(), writes=[tMISC])]
        for e in ("pe", "act", "dve"):
            SC.op(e, None, extra=cl)
        for l in range(L):
            lam_init = 0.8 - 0.6 * math.exp(-0.3 * l)
            t = nt()
            dma("sp", TMP[:, t, 0:256], bass.AP(d_lam, l * 256, [[0, 128], [1, 256]]), writes=[tTMP[t]])
            t2 = nt()
            for j in range(2):
                dve(lambda e, t=t, t2=t2, j=j: e.scalar_tensor_tensor(
                    TMP[:, t2, 0:64], TMP[:, t, 128 * j:128 * j + 64], 1.0, TMP[:, t, 128 * j + 64:128 * j + 128],
                    op0=ALU.mult, op1=ALU.mult, accum_out=SS[:, j:j + 1]), [tTMP[t]], [tTMP[t2], tSS])
            dve(lambda e: e.tensor_copy(SS[:, 15:16], SS[:, 14:15]), [tSS], [tSS])
            act(SS[:, 2:4], SS[:, 0:2], AF.Exp, [tSS], [tSS])
            sop = dve(lambda e: e.tensor_tensor(SS[:, 4:5], SS[:, 3:4], SS[:, 2:3], ALU.subtract), [tSS], [tSS])
            SC.op("dve", lambda e, l=l, lam_init=lam_init: e.tensor_scalar(NLAM[:, l:l + 1], SS[:, 4:5], -lam_init, None,
                                                                           op0=ALU.add), reads=[tSS], writes=[tMISC],
                  force=[sop])
            dve(lambda e, l=l, lam_init=lam_init: e.tensor_scalar(SUBL[:, l:l + 1], PAR[:, P_SUB + l:P_SUB + l + 1],
                                                                  1.0 - lam_init, None, op0=ALU.mult), [], [tMISC])

        for sq in range(nseq):
            wmode["m"] = "direct"
            mem_prep(sq)
            for l in range(L):
                for c in range(8):
                    dve(lambda e, l=l, c=c: e.memset(CARRY[:, l, c, :], 0.0), [], [tCARRY[l][c]])
            for ti in range(ntile):
                t0 = ti * TT
                wmode["m"] = "fill" if (sq == 0 and ti == 0) else "reuse"
                wmode["blk"] = 0
                dma("sp", H[:, :, :], xT.ap()[sq].rearrange("(k p) n -> p k n", p=128)[:, :, t0:t0 + TT], writes=tH)
                for l in range(L):
                    wmode["l"] = l
                    wmode["blk"] = 0
                    if "f" in SUBS:
                        ffn(l, w_f1i, w_f1o, 0)
                    if "m" in SUBS:
                        mixer(l, sq, ti)
                    if "x" in SUBS:
                        xattn(l, sq)
                    if "g" in SUBS:
                        ffn(l, w_f2i, w_f2o, 4)
                rmsnorm(lambda c: PAR[:, P_FN + c:P_FN + c + 1], final=True)
                out_dmas.append(dma("sp", yT.ap()[sq].rearrange("(k p) n -> p k n", p=128)[:, :, t0:t0 + TT],
                                    H[:, :, :], reads=tH))
        SC.op("sp", None, extra=out_dmas)

        SC.prepare()
        with nc.Block() as block:
            @block.tensor
            def _pe(e):
                SC.emit_engine("pe", e, sems, dsems)

            @block.scalar
            def _act(e):
                SC.emit_engine("act", e, sems, dsems)

            @block.vector
            def _dve(e):
                SC.emit_engine("dve", e, sems, dsems)

            @block.gpsimd
            def _pool(e):
                SC.emit_engine("pool", e, sems, dsems)

            @block.sync
            def _sp(e):
                SC.emit_engine("sp", e, sems, dsems)
    return nc


def _consts():
    bf = ml_dtypes.bfloat16
    cbf = np.zeros((128, 5, 128), np.float32)
    cbf[:, 0, :] = np.eye(128)
    cbf[:, 1, :] = 1.0 / 2048
    cbf[:, 2, :] = 1.0 / 128
    cbf[:, 3, :] = 1.0
    k = np.arange(128)[:, None]
    q = np.arange(128)[None, :]
    cbf[:, 4, :] = np.where(k > q, -30000.0, 0.0)
    cf = np.zeros((128, 2, 128), np.float32)
    cf[:, 0, :] = np.eye(128)
    cf[:, 1, :] = (k <= q).astype(np.float32)
    pos = np.arange(S)
    kaug = np.stack([256.0 * (pos // 256), (pos % 256).astype(np.float64), np.ones(S), np.ones(S)]).astype(np.float32)
    slopes = 2.0 ** (-8.0 * np.arange(1, 9) / 8)
    qaug = np.zeros((4, 8, S), np.float32)
    for h in range(8):
        qaug[0, h] = slopes[h]
        qaug[1, h] = slopes[h]
        qaug[2, h] = -slopes[h] * 256.0 * (pos // 256)
        qaug[3, h] = -slopes[h] * (pos % 256)
    return cbf.astype(bf), cf, kaug.astype(bf), qaug.astype(bf)


def _params(inp):
    P = np.zeros((128, P_TOT), np.float32)
    names = ["ffn1_norm", "mix_norm", "xattn_norm", "mem_norm", "ffn2_norm"]
    for l in range(DEPTH):
        for n, nm in enumerate(names):
            P[:, P_GN + l * 80 + n * 16: P_GN + l * 80 + (n + 1) * 16] = np.asarray(inp[nm][l]).reshape(16, 128).T
        for tap in range(3):
            P[:, P_CW + l * 24 + tap * 8: P_CW + l * 24 + (tap + 1) * 8] = np.asarray(inp["conv_w"][l, tap]).reshape(8, 128).T
        P[:, P_SUB + l] = np.asarray(inp["diff_subln"][l])
    P[:, P_FN:P_FN + 16] = np.asarray(inp["final_norm"]).reshape(16, 128).T
    P[:, P_EPS] = EPS
    return P


_NC_CACHE = {}


def run(inputs, ncores=8, nseq=2, ntile=4, nlayer=DEPTH):
    key = (nseq, ntile, nlayer)
    if key not in _NC_CACHE:
        _NC_CACHE[key] = build_program(nseq, ntile, nlayer)
    nc = _NC_CACHE[key]
    inp = {k: np.asarray(v) for k, v in inputs.items()}
    cbf, cf, kaug, qaug = _consts()
    shared = {
        "ffn1_w_in": inp["ffn1_w_in"], "ffn1_w_out": inp["ffn1_w_out"],
        "ffn2_w_in": inp["ffn2_w_in"], "ffn2_w_out": inp["ffn2_w_out"],
        "mix_w_in": inp["mix_w_in"], "diff_w_out": inp["diff_w_out"], "sgu_w_out": inp["sgu_w_out"],
        "conv_w_out": inp["conv_w_out"], "mix_w_o": inp["mix_w_o"], "xattn_w_q": inp["xattn_w_q"],
        "xattn_w_kv": inp["xattn_w_kv"], "xattn_w_o": inp["xattn_w_o"],
        "diff_lambda": np.ascontiguousarray(inp["diff_lambda"].reshape(DEPTH, 256)),
        "sgu_norm": inp["sgu_norm"], "sgu_b": np.ascontiguousarray(inp["sgu_b"].reshape(DEPTH, 1024)),
        "sgu_w_s": inp["sgu_w_s"], "params": _params(inp), "cbf": cbf, "cf32": cf, "kaug": kaug, "qaug": qaug,
    }
    in_maps = []
    for c in range(ncores):
        xs = inp["x"][c * nseq:(c + 1) * nseq]
        ms = inp["mem"][c * nseq:(c + 1) * nseq]
        m = dict(shared)
        m["xT"] = np.ascontiguousarray(xs.transpose(0, 2, 1))
        m["memT"] = np.ascontiguousarray(ms.transpose(0, 2, 1))
        in_maps.append(m)
    res = run_bass_kernel_spmd(nc, in_maps, core_ids=list(range(ncores)))
    outs = [np.asarray(r["yT"]).transpose(0, 2, 1) for r in res.results]
    if DEBUG:
        run.dbg = (np.asarray(res.results[0]["dbgf"]), np.asarray(res.results[0]["dbgb"]).astype(np.float32))
    return np.ascontiguousarray(np.concatenate(outs, axis=0)).astype(np.float32)


def kernel(**inputs):
    return run(inputs)
```

```python
import math
from contextlib import ExitStack
import numpy as np
import ml_dtypes
import concourse.bass as bass
import concourse.mybir as mybir
from concourse.bass_utils import run_bass_kernel_spmd

F32 = mybir.dt.float32
BF16 = mybir.dt.bfloat16
AF = mybir.ActivationFunctionType
ALU = mybir.AluOpType

D = 2048
S = 2048
DEPTH = 4
NMEM = 256
DFF = 5632
MIXW = 14336
TT = 512
NCH = D // 128
EPS = 1e-6
NDS = 48
NSLOT = 4
SLOTE = 4096
NTMP = 5
NPT = 4
ENG = ("pe", "act", "dve", "pool", "sp")
SUBS = "fmxg"
DEBUG = False
NDBG = 24
P_GN, P_FN, P_CW, P_SUB, P_EPS, P_TOT = 0, 320, 336, 432, 436, 437


class Tok:
    __slots__ = ("w", "rc", "rd")

    def __init__(self):
        self.w = None
        self.rc = {}
        self.rd = []


class Op:
    __slots__ = ("eng", "fn", "deps", "sig", "sidx", "dma", "dsem", "dval")


class Sched:
    def __init__(self):
        self.ops = {e: [] for e in ENG}
        self.ndma = 0
        self.dma_ops = []

    def op(self, eng, fn, reads=(), writes=(), dma=False, extra=(), force=()):
        o = Op()
        o.eng = eng
        o.fn = fn
        o.sig = False
        o.dma = dma
        o.sidx = 0
        deps = list(extra) + list(force)
        for t in reads:
            if t.w is not None:
                deps.append(t.w)
        for t in writes:
            if t.w is not None:
                deps.append(t.w)
            deps.extend(t.rc.values())
            deps.extend(t.rd)
        if dma:
            n = self.ndma
            self.ndma += 1
            o.dsem = n % NDS
            o.dval = 16 * (n // NDS + 1)
            if n >= NDS:
                deps.append(self.dma_ops[n - NDS])
            self.dma_ops.append(o)
        dd = []
        seen = set()
        for d in deps:
            if d is o or id(d) in seen:
                continue
            seen.add(id(d))
            if (not d.dma) and d.eng == eng and not any(d is f for f in force):
                continue
            if not d.dma:
                d.sig = True
            dd.append(d)
        o.deps = dd
        for t in reads:
            if dma:
                t.rd.append(o)
            else:
                t.rc[eng] = o
        for t in writes:
            t.w = o
            t.rc = {}
            t.rd = []
        self.ops[eng].append(o)
        return o

    def prepare(self):
        for e in ENG:
            c = 0
            for o in self.ops[e]:
                if o.sig:
                    c += 1
                    o.sidx = c

    def emit_engine(self, e, eng, sems, dsems):
        waited = {}
        for o in self.ops[e]:
            for d in o.deps:
                if d.dma:
                    key, val, sem = ("d", d.dsem), d.dval, dsems[d.dsem]
                else:
                    key, val, sem = d.eng, d.sidx, sems[d.eng]
                if waited.get(key, 0) >= val:
                    continue
                waited[key] = val
                eng.wait_ge(sem, val)
            last = o.fn(eng) if o.fn is not None else None
            if o.dma:
                last.then_inc(dsems[o.dsem], 16)
            elif o.sig:
                last.then_inc(sems[e], 1)


def build_program(nseq=2, ntile=4, nlayer=DEPTH):
    nc = bass.Bass("TRN2", target_bir_lowering=False)
    L = nlayer
    dt = nc.dram_tensor
    xT = dt("xT", [nseq, D, S], F32, kind="ExternalInput")
    memT = dt("memT", [nseq, D, NMEM], F32, kind="ExternalInput")
    yT = dt("yT", [nseq, D, S], F32, kind="ExternalOutput")
    w_f1i = dt("ffn1_w_in", [DEPTH, D, 2 * DFF], F32, kind="ExternalInput")
    w_f1o = dt("ffn1_w_out", [DEPTH, DFF, D], F32, kind="ExternalInput")
    w_f2i = dt("ffn2_w_in", [DEPTH, D, 2 * DFF], F32, kind="ExternalInput")
    w_f2o = dt("ffn2_w_out", [DEPTH, DFF, D], F32, kind="ExternalInput")
    w_mix = dt("mix_w_in", [DEPTH, D, MIXW], F32, kind="ExternalInput")
    w_dout = dt("diff_w_out", [DEPTH, 1024, D], F32, kind="ExternalInput")
    w_sout = dt("sgu_w_out", [DEPTH, 1024, D], F32, kind="ExternalInput")
    w_cout = dt("conv_w_out", [DEPTH, 1024, D], F32, kind="ExternalInput")
    w_mo = dt("mix_w_o", [DEPTH, D, D], F32, kind="ExternalInput")
    w_xq = dt("xattn_w_q", [DEPTH, D, 512], F32, kind="ExternalInput")
    w_xkv = dt("xattn_w_kv", [DEPTH, D, 1024], F32, kind="ExternalInput")
    w_xo = dt("xattn_w_o", [DEPTH, 512, D], F32, kind="ExternalInput")
    d_lam = dt("diff_lambda", [DEPTH, 256], F32, kind="ExternalInput")
    d_sgn = dt("sgu_norm", [DEPTH, 1024], F32, kind="ExternalInput")
    d_sgb = dt("sgu_b", [DEPTH, 1024], F32, kind="ExternalInput")
    d_sws = dt("sgu_w_s", [DEPTH, 8, 128, 128], F32, kind="ExternalInput")
    d_par = dt("params", [128, P_TOT], F32, kind="ExternalInput")
    d_cbf = dt("cbf", [128, 5, 128], BF16, kind="ExternalInput")
    d_cf = dt("cf32", [128, 2, 128], F32, kind="ExternalInput")
    d_kaug = dt("kaug", [4, S], BF16, kind="ExternalInput")
    d_qaug = dt("qaug", [4, 8, S], BF16, kind="ExternalInput")
    kc = dt("kc", [DEPTH, nseq, 8, 128, S], BF16, kind="Internal")
    vc = dt("vc", [DEPTH, nseq, 8, 128, S // 128, 128], BF16, kind="Internal")
    mkd = dt("mkd", [DEPTH, nseq, 128, 4 * NMEM], BF16, kind="Internal")
    mvd = dt("mvd", [DEPTH, nseq, 128, 2 * 512], BF16, kind="Internal")
    NBLK = 232
    wscs = [dt("wsc%d" % l_, [NBLK, 128, SLOTE], BF16, kind="Internal") for l_ in range(DEPTH)]

    if DEBUG:
        dbgf = dt("dbgf", [NDBG, 128, 516], F32, kind="ExternalOutput")
        dbgb = dt("dbgb", [NDBG, 128, 512], BF16, kind="ExternalOutput")
    dbgn = {"f": 0, "b": 0}
    SC = Sched()
    es = ExitStack()
    with es:
        def sb(name, shape, dtype):
            return es.enter_context(nc.sbuf_tensor(name, shape, dtype))

        H = sb("H", [128, NCH, TT], F32)
        XN = sb("XN", [128, NCH, TT], BF16)
        BIG = sb("BIG", [128, 44, TT], BF16)
        WR = sb("WR", [128, NSLOT, SLOTE], BF16)
        RB = sb("RB", [128, TT], F32)
        TMP = sb("TMP", [128, NTMP, 516], F32)
        ACC = sb("ACC", [128, 2, TT], F32)
        PT = sb("PT", [128, NPT, TT], BF16)
        KH = sb("KH", [128, 2, S], BF16)
        VH = sb("VH", [128, 2, 16, 128], BF16)
        QA = sb("QA", [128, 2, TT], BF16)
        KAUG = sb("KAUG", [128, S], BF16)
        MK = sb("MK", [128, 4, NMEM], BF16)
        MV = sb("MV", [128, 2, 512], BF16)
        GB = sb("GB", [128, 2, 1024], F32)
        WSR = sb("WSR", [128, 8, 128], F32)
        WS = sb("WS", [128, 8, 128], BF16)
        PAR = sb("PAR", [128, P_TOT], F32)
        CBF = sb("CBF", [128, 5, 128], BF16)
        CF = sb("CF", [128, 2, 128], F32)
        NLAM = sb("NLAM", [128, DEPTH], F32)
        SUBL = sb("SUBL", [128, DEPTH], F32)
        CARRY = sb("CARRY", [128, DEPTH, 8, 2], F32)
        SS = sb("SS", [128, 16], F32)
        PS = es.enter_context(nc.psum_tensor("PS", [128, 8, 512], F32))
        sems = {e: es.enter_context(nc.semaphore("s_" + e)) for e in ENG}
        dsems = [es.enter_context(nc.semaphore("d%d" % i)) for i in range(NDS)]

        IDB = CBF[:, 0, :]
        ONE_D = CBF[:, 1, :]
        ONE_H = CBF[:, 2, :]
        ONE_1 = CBF[:, 3, :]
        CMASK = CBF[:, 4, :]
        IDF = CF[:, 0, :]
        TRIL = CF[:, 1, :]

        tH = [Tok() for _ in range(NCH)]
        tXN = [Tok() for _ in range(NCH)]
        tBIG = [Tok() for _ in range(44)]
        tWR = [Tok() for _ in range(NSLOT)]
        tRB = Tok()
        tTMP = [Tok() for _ in range(NTMP)]
        tPT = [Tok() for _ in range(NPT)]
        tKH = [Tok(), Tok()]
        tVH = [Tok(), Tok()]
        tQA = [Tok(), Tok()]
        tQA2 = [Tok(), Tok()]
        tMK, tMV, tGB, tWSR, tWS = Tok(), Tok(), Tok(), Tok(), Tok()
        tACC = [Tok(), Tok()]
        tCARRY = [[Tok() for _ in range(8)] for _ in range(DEPTH)]
        tSS = Tok()
        tPS = [Tok() for _ in range(8)]
        tKC = {}
        tVC = {}
        tMKD = {}
        tMVD = {}
        tMISC = Tok()
        tMISC2 = Tok()

        cnt = {"b": 0, "t": 0, "p": 0, "w": 0, "z": 0, "h": 0}

        def nb():
            cnt["b"] = (cnt["b"] + 1) % 8
            return cnt["b"]

        def nt():
            cnt["t"] = (cnt["t"] + 1) % NTMP
            return cnt["t"]

        def npt():
            cnt["p"] = (cnt["p"] + 1) % NPT
            return cnt["p"]

        def gn(l, n, c):
            i = P_GN + l * 80 + n * 16 + c
            return PAR[:, i:i + 1]

        out_dmas = []
        def dumpf(name, ap, n, reads):
            if not DEBUG or dbgn["f"] >= NDBG:
                return
            i = dbgn["f"]
            dbgn["f"] += 1
            print("DUMPF", i, name)
            out_dmas.append(dma("sp", dbgf.ap()[i, :, 0:n], ap, reads=reads))

        def dumpb(name, ap, n, reads):
            if not DEBUG or dbgn["b"] >= NDBG:
                return
            i = dbgn["b"]
            dbgn["b"] += 1
            print("DUMPB", i, name)
            out_dmas.append(dma("sp", dbgb.ap()[i, :, 0:n], ap, reads=reads))

        def dma(eng, out, in_, reads=(), writes=()):
            return SC.op(eng, lambda e: e.dma_start(out=out, in_=in_), reads=reads, writes=writes, dma=True)

        wmode = {"m": "direct", "blk": 0}
        tWSC = {}

        def wload(src, kcn, ncols):
            s = cnt["w"]
            cnt["w"] = (s + 1) % NSLOT
            E = kcn * ncols
            dst = WR[:, s, 0:E].rearrange("p (k n) -> p k n", k=kcn)
            if wmode["m"] == "direct":
                dma("pool", dst, src, writes=[tWR[s]])
                return s, dst
            blk = wmode["blk"]
            wmode["blk"] += 1
            wsc = wscs[wmode["l"]]
            blk_key = (wmode["l"], blk)
            if wmode["m"] == "fill":
                dma("pool", dst, src, writes=[tWR[s]])
                tWSC[blk_key] = Tok()
                dma("sp", wsc.ap()[blk, :, 0:E], WR[:, s, 0:E], reads=[tWR[s]], writes=[tWSC[blk_key]])
            else:
                dma("pool", WR[:, s, 0:E], wsc.ap()[blk, :, 0:E], reads=[tWSC[blk_key]], writes=[tWR[s]])
            return s, dst

        fresh = {"n": 0}

        def mm(out, pairs, reads, wtok, first=True, last=True):
            n = len(pairs)
            if (fresh["n"] > 0 and n == NCH and first and last and len(reads) >= NCH
                    and all(a is b for a, b in zip(reads[-NCH:], tXN))):
                fresh["n"] -= 1
                base = list(reads[:-NCH])
                o = None
                for k, (lt, rh) in enumerate(pairs):
                    o = SC.op("pe", lambda pe, lt=lt, rh=rh, k=k: pe.matmul(out, lt, rh, start=(k == 0), stop=(k == n - 1)),
                              reads=base + [tXN[k]], writes=[wtok])
                return o

            def fn(pe):
                ins = None
                for i, (lt, rh) in enumerate(pairs):
                    ins = pe.matmul(out, lt, rh, start=(first and i == 0), stop=(last and i == n - 1))
                return ins
            return SC.op("pe", fn, reads=reads, writes=[wtok])

        def act(out, in_, func, reads, writes, scale=1.0):
            return SC.op("act", lambda e: e.activation(out, in_, func, scale=scale), reads=reads, writes=writes)

        def dve(fn, reads, writes):
            return SC.op("dve", fn, reads=reads, writes=writes)

        def tt(out, a, b, op, reads, writes):
            return dve(lambda e: e.tensor_tensor(out, a, b, op), reads, writes)

        def rsqrt(out, in_, reads, writes, scale=1.0):
            SC.op("act", lambda e: e.activation(out, in_, AF.Sqrt, bias=PAR[:, P_EPS:P_EPS + 1], scale=scale),
                  reads=reads, writes=writes)
            return dve(lambda e: e.reciprocal(out, out), writes, writes)

        def rmsnorm(gain, n=TT, final=False):
            for c in range(NCH):
                act(XN[:, c, 0:n], H[:, c, 0:n], AF.Square, [tH[c]], [tXN[c]])
            b = nb()
            mm(PS[:, b, 0:n], [(ONE_D, XN[:, c, 0:n]) for c in range(NCH)], tXN, tPS[b])
            rsqrt(RB[:, 0:n], PS[:, b, 0:n], [tPS[b]], [tRB])
            if gain is None:
                return
            if not final:
                fresh["n"] = 2
            for c in range(NCH):
                o = H[:, c, 0:n] if final else XN[:, c, 0:n]
                dve(lambda e, o=o, c=c: e.scalar_tensor_tensor(o, H[:, c, 0:n], gain(c), RB[:, 0:n],
                                                               op0=ALU.mult, op1=ALU.mult),
                    [tH[c], tRB], [tH[c]] if final else [tXN[c]])

        def proj_fm(wv, col0, nchunks, kcn, rhs_fn, rhs_toks, evac, n=TT):
            j = 0
            while j < nchunks:
                nsub = min(2, nchunks - j)
                s, wv_s = wload(wv[:, 0:kcn, col0 + j * 128: col0 + (j + nsub) * 128], kcn, nsub * 128)
                for sub in range(nsub):
                    b = nb()
                    mm(PS[:, b, 0:n], [(wv_s[:, k, sub * 128:(sub + 1) * 128], rhs_fn(k)) for k in range(kcn)],
                       [tWR[s]] + rhs_toks, tPS[b])
                    evac(j + sub, b)
                j += nsub

        def proj_tm(wv, col0, ncb, lhs_fn, lhs_toks, ntc, evac):
            for cb in range(ncb):
                sl = []
                for kh in range(2):
                    sl.append(wload(wv[:, kh * 8:(kh + 1) * 8, col0 + cb * 512: col0 + (cb + 1) * 512], 8, 512))
                for tc in range(ntc):
                    b = nb()
                    mm(PS[:, b, :], [(lhs_fn(k, tc), sl[k // 8][1][:, k % 8, :]) for k in range(NCH)],
                       [tWR[sl[0][0]], tWR[sl[1][0]]] + lhs_toks, tPS[b])
                    evac(tc, cb, b)

        def ffn(l, w_in, w_out, nidx):
            wi = w_in.ap()[l].rearrange("(k p) n -> p k n", p=128)
            wo = w_out.ap()[l].rearrange("(k p) n -> p k n", p=128)
            rmsnorm(lambda c: gn(l, nidx, c))
            for jj in range(22):
                sa, va = wload(wi[:, :, jj * 256:(jj + 1) * 256], 16, 256)
                sb_, vb = wload(wi[:, :, DFF + jj * 256: DFF + (jj + 1) * 256], 16, 256)
                for sub in range(2):
                    j = 2 * jj + sub
                    ba = nb()
                    mm(PS[:, ba, :], [(va[:, k, sub * 128:(sub + 1) * 128], XN[:, k, :]) for k in range(NCH)],
                       [tWR[sa]] + tXN, tPS[ba])
                    bb = nb()
                    mm(PS[:, bb, :], [(vb[:, k, sub * 128:(sub + 1) * 128], XN[:, k, :]) for k in range(NCH)],
                       [tWR[sb_]] + tXN, tPS[bb])
                    t = nt()
                    act(TMP[:, t, 0:TT], PS[:, ba, :], AF.Silu, [tPS[ba]], [tTMP[t]])
                    tt(BIG[:, j, :], TMP[:, t, 0:TT], PS[:, bb, :], ALU.mult, [tTMP[t], tPS[bb]], [tBIG[j]])
            for qd in range(4):
                banks = [nb() for _ in range(4)]
                for ks in range(6):
                    k0 = ks * 8
                    kn = min(8, 44 - k0)
                    s, wv_s = wload(wo[:, k0:k0 + kn, qd * 512:(qd + 1) * 512], kn, 512)
                    for mi in range(4):
                        mm(PS[:, banks[mi], :],
                           [(wv_s[:, kk, mi * 128:(mi + 1) * 128], BIG[:, k0 + kk, :]) for kk in range(kn)],
                           [tWR[s]] + tBIG[k0:k0 + kn], tPS[banks[mi]], first=(ks == 0), last=(ks == 5))
                for mi in range(4):
                    c = qd * 4 + mi
                    b = banks[mi]
                    dve(lambda e, c=c, b=b: e.scalar_tensor_tensor(H[:, c, :], PS[:, b, :], 0.5, H[:, c, :],
                                                                   op0=ALU.mult, op1=ALU.add),
                        [tPS[b], tH[c]], [tH[c]])

        def mixer(l, sq, ti):
            t0 = ti * TT
            wv = w_mix.ap()[l].rearrange("(k p) n -> p k n", p=128)
            rmsnorm(lambda c: gn(l, 1, c))
            xn_rhs = lambda k: XN[:, k, :]
            dma("sp", GB[:, 0, :], bass.AP(d_sgn, l * 1024, [[0, 128], [1, 1024]]), writes=[tGB])
            tGB2 = tMISC2
            dma("sp", GB[:, 1, :], bass.AP(d_sgb, l * 1024, [[0, 128], [1, 1024]]), writes=[tGB2])
            gb_ops = list(SC.dma_ops[-2:])
            dma("sp", WSR[:, :, :], d_sws.ap()[l].rearrange("g t s -> t g s"), writes=[tWSR])

            def ev_k(h, b):
                act(BIG[:, 24 + h, :], PS[:, b, :], AF.Copy, [tPS[b]], [tBIG[24 + h]])
                tk = Tok()
                tKC[(l, sq, ti, h)] = tk
                dma("sp", kc.ap()[l, sq, h, :, t0:t0 + TT], BIG[:, 24 + h, :], reads=[tBIG[24 + h]], writes=[tk])
            proj_fm(wv, 1024, 8, 16, xn_rhs, tXN, ev_k)

            def ev_v(tc, cb, b):
                ci = 8 + tc * 2 + cb
                act(BIG[:, ci, :], PS[:, b, :], AF.Copy, [tPS[b]], [tBIG[ci]])
            proj_tm(wv, 2048, 2, lambda k, tc: XN[:, k, tc * 128:(tc + 1) * 128], tXN, 4, ev_v)
            for tc in range(4):
                tk = Tok()
                tVC[(l, sq, ti, tc)] = tk
                src = BIG[:, 8 + 2 * tc: 10 + 2 * tc, :].rearrange("p a (h d) -> p (a h) d", d=128)
                dst = vc.ap()[l, sq, :, :, t0 // 128 + tc, :].rearrange("h p d -> p h d")
                dma("sp", dst, src, reads=[tBIG[8 + 2 * tc], tBIG[9 + 2 * tc]], writes=[tk])

            def ev_q(h, b):
                act(BIG[:, h, :], PS[:, b, :], AF.Copy, [tPS[b]], [tBIG[h]], scale=0.125)
            proj_fm(wv, 0, 8, 16, xn_rhs, tXN, ev_q)

            def ev_u(c, b):
                act(BIG[:, 16 + c, :], PS[:, b, :], AF.Gelu, [tPS[b]], [tBIG[16 + c]])
            proj_fm(wv, 3072, 8, 16, xn_rhs, tXN, ev_u)

            def ev_gv(tc, cb, b):
                ci = 24 + tc * 2 + cb
                act(BIG[:, ci, :], PS[:, b, :], AF.Gelu, [tPS[b]], [tBIG[ci]])
                t = nt()
                col = tc * 2 + cb
                dve(lambda e: e.scalar_tensor_tensor(TMP[:, t, 0:TT], BIG[:, ci, :], 1.0, BIG[:, ci, :],
                                                     op0=ALU.mult, op1=ALU.mult, accum_out=SS[:, col:col + 1]),
                    [tBIG[ci]], [tTMP[t], tSS])
            proj_tm(wv, 4096, 2, lambda k, tc: XN[:, k, tc * 128:(tc + 1) * 128], tXN, 4, ev_gv)
            sep = dve(lambda e: e.tensor_copy(SS[:, 15:16], SS[:, 14:15]), [tSS], [tSS])
            for tc in range(4):
                SC.op("dve", lambda e, tc=tc: e.tensor_tensor(SS[:, 8 + tc:9 + tc], SS[:, 2 * tc:2 * tc + 1],
                                                              SS[:, 2 * tc + 1:2 * tc + 2], ALU.add),
                      reads=[tSS], writes=[tSS], force=[sep])
                rop = rsqrt(SS[:, 12 + tc:13 + tc], SS[:, 8 + tc:9 + tc], [tSS], [tSS], scale=1.0 / 1024)
                for cb in range(2):
                    ci = 24 + tc * 2 + cb
                    SC.op("dve", lambda e, tc=tc, cb=cb, ci=ci: e.scalar_tensor_tensor(
                        BIG[:, ci, :], BIG[:, ci, :], SS[:, 12 + tc:13 + tc], GB[:, 0, cb * 512:(cb + 1) * 512],
                        op0=ALU.mult, op1=ALU.mult), reads=[tBIG[ci], tSS, tGB], writes=[tBIG[ci]], force=[rop])

            for cp in range(4):
                s3 = [wload(wv[:, :, off + cp * 256: off + (cp + 1) * 256], 16, 256) for off in (5120, 6144, 7168)]
                for sub in range(2):
                    c = 2 * cp + sub
                    bs = []
                    for (s, v) in s3:
                        b = nb()
                        mm(PS[:, b, :], [(v[:, k, sub * 128:(sub + 1) * 128], XN[:, k, :]) for k in range(NCH)],
                           [tWR[s]] + tXN, tPS[b])
                        bs.append(b)
                    b_b, b_c, b_x = bs
                    t1 = nt()
                    act(TMP[:, t1, 0:TT], PS[:, b_x, :], AF.Copy, [tPS[b_x]], [tTMP[t1]])
                    z = nt()
                    tt(TMP[:, z, 2:514], PS[:, b_c, :], TMP[:, t1, 0:TT], ALU.mult, [tPS[b_c], tTMP[t1]], [tTMP[z]])
                    cop = dve(lambda e, z=z, c=c: e.tensor_copy(TMP[:, z, 0:2], CARRY[:, l, c, :]), [tCARRY[l][c]], [tTMP[z]])
                    dve(lambda e, z=z, c=c: e.tensor_copy(CARRY[:, l, c, :], TMP[:, z, 512:514]), [tTMP[z]],
                        [tCARRY[l][c]])
                    t2 = nt()
                    cw = lambda tap, c=c: PAR[:, P_CW + l * 24 + tap * 8 + c: P_CW + l * 24 + tap * 8 + c + 1]
                    SC.op("dve", lambda e, z=z, t2=t2, cw=cw: e.tensor_scalar(TMP[:, t2, 0:TT], TMP[:, z, 0:512], cw(0), None,
                                                                              op0=ALU.mult),
                          reads=[tTMP[z]], writes=[tTMP[t2]], force=[cop])
                    for tap in (1, 2):
                        dve(lambda e, z=z, t2=t2, cw=cw, tap=tap: e.scalar_tensor_tensor(
                            TMP[:, t2, 0:TT], TMP[:, z, tap:tap + 512], cw(tap), TMP[:, t2, 0:TT],
                            op0=ALU.mult, op1=ALU.add), [tTMP[z], tTMP[t2]], [tTMP[t2]])
                    if c < 4 and DEBUG:
                        dumpf("Z%d" % c, TMP[:, z, 0:514], 514, [tTMP[z]])
                        dumpf("T2_%d" % c, TMP[:, t2, 0:TT], 512, [tTMP[t2]])
                    tt(BIG[:, 32 + c, :], TMP[:, t2, 0:TT], PS[:, b_b, :], ALU.mult, [tTMP[t2], tPS[b_b]], [tBIG[32 + c]])
                    if DEBUG:
                        dumpb("YC%d" % c, BIG[:, 32 + c, :], 512, [tBIG[32 + c]])

            for half in range(2):
                b = nb()

                def fn(pe, half=half, b=b):
                    ins = None
                    for gi in range(4):
                        ins = pe.transpose(PS[:, b, gi * 128:(gi + 1) * 128], WSR[:, half * 4 + gi, :], IDF)
                    return ins
                SC.op("pe", fn, reads=[tWSR], writes=[tPS[b]])
                for gi in range(4):
                    g = half * 4 + gi
                    tt(WS[:, g, :], PS[:, b, gi * 128:(gi + 1) * 128], TRIL, ALU.mult, [tPS[b]], [tWS])

            for g in range(8):
                b = nb()

                def fn(pe, g=g, b=b):
                    ins = None
                    for pc in range(4):
                        ins = pe.matmul(PS[:, b, pc * 128:(pc + 1) * 128],
                                        BIG[:, 24 + pc * 2 + g // 4, (g % 4) * 128:(g % 4 + 1) * 128],
                                        WS[:, g, :], start=True, stop=True)
                    return ins
                SC.op("pe", fn, reads=[tWS] + tBIG[24:32], writes=[tPS[b]])
                t = nt()
                for pc in range(4):
                    tt(TMP[:, t, pc * 128:(pc + 1) * 128], PS[:, b, pc * 128:(pc + 1) * 128],
                       GB[:, 1, g * 128:(g + 1) * 128], ALU.add, [tPS[b], tGB2], [tTMP[t]])
                tt(BIG[:, 16 + g, :], TMP[:, t, 0:TT], BIG[:, 16 + g, :], ALU.mult, [tTMP[t], tBIG[16 + g]], [tBIG[16 + g]])

            dumpb("WS0", WS[:, 0, :], 128, [tWS])
            dumpb("YB16", BIG[:, 16, :], 512, [tBIG[16]])
            dumpf("GB1", GB[:, 1, 0:512], 512, [tGB2])
            nkb = (t0 + TT) // 128
            nk = t0 + TT
            pending = [None]
            for h in range(8):
                hs = cnt["h"]
                cnt["h"] = 1 - hs
                dma("sp", KH[:, hs, 0:nk], kc.ap()[l, sq, h, :, 0:nk],
                    reads=[tKC[(l, sq, tj, h)] for tj in range(ti + 1)], writes=[tKH[hs]])
                dma("sp", VH[:, hs, 0:nkb, :], vc.ap()[l, sq, h, :, 0:nkb, :],
                    reads=[tVC[(l, sq, tj, tc)] for tj in range(ti + 1) for tc in range(4)], writes=[tVH[hs]])
                dma("sp", QA[0:4, hs, :], d_qaug.ap()[:, h, t0:t0 + TT], writes=[tQA[hs]])
                dma("sp", QA[64:68, hs, :], d_qaug.ap()[:, h, t0:t0 + TT], writes=[tQA2[hs]])
                bo = [0, 1]
                bz = [2, 3]
                unit_p = {}

                def emit_scores(kb, hs=hs, h=h, unit_p=unit_p):
                    r = kb - t0 // 128
                    n0 = 128 * r if r > 0 else 0
                    bs2 = []
                    for i in range(2):
                        cnt["sb"] = (cnt.get("sb", 0) + 1) % 4
                        bs2.append(4 + cnt["sb"])

                    def fn(pe, kb=kb, r=r, n0=n0, bs2=bs2, hs=hs, h=h):
                        for i in range(2):
                            pe.matmul(PS[:, bs2[i], n0:TT], KH[64 * i:64 * i + 64, hs, kb * 128:(kb + 1) * 128],
                                      BIG[64 * i:64 * i + 64, h, n0:TT], start=True, stop=False)
                        if r >= 0:
                            for i in range(2):
                                pe.matmul(PS[:, bs2[i], n0:n0 + 128], IDB, CMASK, start=False, stop=False)
                        ins = None
                        for i in range(2):
                            ins = pe.matmul(PS[:, bs2[i], n0:TT], KAUG[64 * i:64 * i + 4, kb * 128:(kb + 1) * 128],
                                            QA[64 * i:64 * i + 4, hs, n0:TT], start=False, stop=True)
                        return ins
                    SC.op("pe", fn, reads=[tKH[hs], tBIG[h], tQA[hs], tQA2[hs]], writes=[tPS[bs2[0]], tPS[bs2[1]]])
                    for i in range(2):
                        p = npt()
                        act(PT[:, p, n0:TT], PS[:, bs2[i], n0:TT], AF.Exp, [tPS[bs2[i]]], [tPT[p]])
                        unit_p[(kb, i)] = (p, n0)

                def emit_pv(kb, hs=hs, unit_p=unit_p, bo=bo, bz=bz):
                    for i in range(2):
                        p, n0 = unit_p[(kb, i)]

                        def fn2(pe, i=i, kb=kb, n0=n0, p=p, hs=hs, bo=bo, bz=bz):
                            pe.matmul(PS[:, bo[i], n0:TT], VH[:, hs, kb, :], PT[:, p, n0:TT],
                                      start=(kb == 0), stop=(kb == nkb - 1))
                            return pe.matmul(PS[:, bz[i], n0:TT], ONE_1, PT[:, p, n0:TT],
                                             start=(kb == 0), stop=(kb == nkb - 1))
                        SC.op("pe", fn2, reads=[tVH[hs], tPT[p]], writes=[tPS[bo[i]], tPS[bz[i]]])

                emit_scores(0)
                if pending[0] is not None:
                    pending[0]()
                    pending[0] = None
                for kb in range(nkb):
                    if kb + 1 < nkb:
                        emit_scores(kb + 1)
                    emit_pv(kb)
                ta, tb_, tc_ = nt(), nt(), nt()
                dve(lambda e, ta=ta, bz=bz: e.reciprocal(TMP[:, ta, 0:TT], PS[:, bz[0], :]), [tPS[bz[0]]], [tTMP[ta]])
                tt(TMP[:, tb_, 0:TT], PS[:, bo[0], :], TMP[:, ta, 0:TT], ALU.mult, [tPS[bo[0]], tTMP[ta]], [tTMP[tb_]])
                dve(lambda e, ta=ta, bz=bz: e.reciprocal(TMP[:, ta, 0:TT], PS[:, bz[1], :]), [tPS[bz[1]]], [tTMP[ta]])
                tt(TMP[:, tc_, 0:TT], PS[:, bo[1], :], TMP[:, ta, 0:TT], ALU.mult, [tPS[bo[1]], tTMP[ta]], [tTMP[tc_]])
                dve(lambda e, tb_=tb_, tc_=tc_: e.scalar_tensor_tensor(TMP[:, tb_, 0:TT], TMP[:, tc_, 0:TT], NLAM[:, l:l + 1],
                                                                       TMP[:, tb_, 0:TT], op0=ALU.mult, op1=ALU.add),
                    [tTMP[tb_], tTMP[tc_]], [tTMP[tb_]])
                p = npt()
                act(PT[:, p, :], TMP[:, tb_, 0:TT], AF.Square, [tTMP[tb_]], [tPT[p]])

                def fin(h=h, p=p, ta=ta, tb_=tb_):
                    cnt["sb"] = (cnt.get("sb", 0) + 1) % 4
                    b = 4 + cnt["sb"]
                    mm(PS[:, b, :], [(ONE_H, PT[:, p, :])], [tPT[p]], tPS[b])
                    rsqrt(TMP[:, ta, 0:TT], PS[:, b, :], [tPS[b]], [tTMP[ta]])
                    dve(lambda e: e.scalar_tensor_tensor(BIG[:, 8 + h, :], TMP[:, tb_, 0:TT], SUBL[:, l:l + 1],
                                                         TMP[:, ta, 0:TT], op0=ALU.mult, op1=ALU.mult),
                        [tTMP[ta], tTMP[tb_]], [tBIG[8 + h]])
                pending[0] = fin
            pending[0]()
            pending[0] = None

            bw = [w_dout.ap()[l].rearrange("(k p) n -> p k n", p=128),
                  w_sout.ap()[l].rearrange("(k p) n -> p k n", p=128),
                  w_cout.ap()[l].rearrange("(k p) n -> p k n", p=128)]
            ybase = [8, 16, 32]
            mch = lambda m: m if m < 8 else 24 + (m - 8)
            for mp in range(8):
                for br in range(3):
                    sg, vg = wload(wv[:, :, 8192 + br * 2048 + mp * 256: 8192 + br * 2048 + (mp + 1) * 256], 16, 256)
                    sw, vw = wload(bw[br][:, :, mp * 256:(mp + 1) * 256], 8, 256)
                    yb = ybase[br]
                    for sub in range(2):
                        m = 2 * mp + sub
                        b = nb()
                        mm(PS[:, b, :], [(vg[:, k, sub * 128:(sub + 1) * 128], XN[:, k, :]) for k in range(NCH)],
                           [tWR[sg]] + tXN, tPS[b])
                        t = nt()
                        act(TMP[:, t, 0:TT], PS[:, b, :], AF.Sigmoid, [tPS[b]], [tTMP[t]])
                        b2 = nb()
                        mm(PS[:, b2, :], [(vw[:, k, sub * 128:(sub + 1) * 128], BIG[:, yb + k, :]) for k in range(8)],
                           [tWR[sw]] + tBIG[yb:yb + 8], tPS[b2])
                        if br == 0:
                            tt(ACC[:, sub, :], TMP[:, t, 0:TT], PS[:, b2, :], ALU.mult, [tTMP[t], tPS[b2]], [tACC[sub]])
                        else:
                            tt(TMP[:, t, 0:TT], TMP[:, t, 0:TT], PS[:, b2, :], ALU.mult, [tTMP[t], tPS[b2]], [tTMP[t]])
                            if br == 1:
                                tt(ACC[:, sub, :], ACC[:, sub, :], TMP[:, t, 0:TT], ALU.add, [tACC[sub], tTMP[t]],
                                   [tACC[sub]])
                            else:
                                tt(BIG[:, mch(m), :], ACC[:, sub, :], TMP[:, t, 0:TT], ALU.add, [tACC[sub], tTMP[t]],
                                   [tBIG[mch(m)]])

            wo = w_mo.ap()[l].rearrange("(k p) n -> p k n", p=128)

            def ev_o(c, b):
                tt(H[:, c, :], PS[:, b, :], H[:, c, :], ALU.add, [tPS[b], tH[c]], [tH[c]])
            proj_fm(wo, 0, 16, 16, lambda k: BIG[:, mch(k), :], [tBIG[mch(k)] for k in range(16)], ev_o)

        def xattn(l, sq):
            rmsnorm(lambda c: gn(l, 2, c))
            dma("sp", MK[:, :, :], mkd.ap()[l, sq].rearrange("p (h m) -> p h m", h=4), reads=[tMKD[(l, sq)]],
                writes=[tMK])
            dma("sp", MV[:, :, :], mvd.ap()[l, sq].rearrange("p (b n) -> p b n", b=2), reads=[tMVD[(l, sq)]],
                writes=[tMV])
            wq = w_xq.ap()[l].rearrange("(k p) n -> p k n", p=128)

            def ev_q(h, b):
                act(BIG[:, h, :], PS[:, b, :], AF.Copy, [tPS[b]], [tBIG[h]])
            proj_fm(wq, 0, 4, 16, lambda k: XN[:, k, :], tXN, ev_q)
            sc = 1.0 / math.sqrt(128.0)
            for h in range(4):
                bo, bz = nb(), nb()
                for mb in range(2):
                    bsc = nb()
                    mm(PS[:, bsc, :], [(MK[:, h, mb * 128:(mb + 1) * 128], BIG[:, h, :])], [tMK, tBIG[h]], tPS[bsc])
                    p = npt()
                    act(PT[:, p, :], PS[:, bsc, :], AF.Exp, [tPS[bsc]], [tPT[p]], scale=sc)

                    def fn2(pe, h=h, mb=mb, p=p, bo=bo, bz=bz):
                        pe.matmul(PS[:, bo, :], MV[:, mb, h * 128:(h + 1) * 128], PT[:, p, :], start=(mb == 0),
                                  stop=(mb == 1))
                        return pe.matmul(PS[:, bz, :], ONE_1, PT[:, p, :], start=(mb == 0), stop=(mb == 1))
                    SC.op("pe", fn2, reads=[tMV, tPT[p]], writes=[tPS[bo], tPS[bz]])
                ta = nt()
                dve(lambda e, ta=ta, bz=bz: e.reciprocal(TMP[:, ta, 0:TT], PS[:, bz, :]), [tPS[bz]], [tTMP[ta]])
                tt(BIG[:, 8 + h, :], PS[:, bo, :], TMP[:, ta, 0:TT], ALU.mult, [tPS[bo], tTMP[ta]], [tBIG[8 + h]])
            wo = w_xo.ap()[l].rearrange("(k p) n -> p k n", p=128)
            for blk in range(4):
                s, v = wload(wo[:, :, blk * 512:(blk + 1) * 512], 4, 512)
                for sub in range(4):
                    c = blk * 4 + sub
                    b = nb()
                    mm(PS[:, b, :], [(v[:, k, sub * 128:(sub + 1) * 128], BIG[:, 8 + k, :]) for k in range(4)],
                       [tWR[s]] + tBIG[8:12], tPS[b])
                    tt(H[:, c, :], PS[:, b, :], H[:, c, :], ALU.add, [tPS[b], tH[c]], [tH[c]])

        def mem_prep(sq):
            n = NMEM
            dma("sp", H[:, :, 0:n], memT.ap()[sq].rearrange("(k p) n -> p k n", p=128), writes=tH)
            rmsnorm(None, n=n)
            for l in range(L):
                for c in range(NCH):
                    dve(lambda e, c=c, l=l: e.scalar_tensor_tensor(XN[:, c, 0:n], H[:, c, 0:n], gn(l, 3, c), RB[:, 0:n],
                                                                   op0=ALU.mult, op1=ALU.mult), [tH[c], tRB], [tXN[c]])
                wkv = w_xkv.ap()[l].rearrange("(k p) n -> p k n", p=128)

                def ev_k(h, b):
                    act(MK[:, h, :], PS[:, b, 0:n], AF.Copy, [tPS[b]], [tMK])
                proj_fm(wkv, 0, 4, 16, lambda k: XN[:, k, 0:n], tXN, ev_k, n=n)

                def ev_v(tc, cb, b):
                    act(MV[:, tc, :], PS[:, b, :], AF.Copy, [tPS[b]], [tMV])
                proj_tm(wkv, 512, 1, lambda k, tc: XN[:, k, tc * 128:(tc + 1) * 128], tXN, 2, ev_v)
                tMKD[(l, sq)] = Tok()
                tMVD[(l, sq)] = Tok()
                dma("sp", mkd.ap()[l, sq].rearrange("p (h m) -> p h m", h=4), MK[:, :, :], reads=[tMK],
                    writes=[tMKD[(l, sq)]])
                dma("sp", mvd.ap()[l, sq].rearrange("p (b n) -> p b n", b=2), MV[:, :, :], reads=[tMV],
                    writes=[tMVD[(l, sq)]])

        cl = [dma("sp", PAR[:, :], d_par.ap(), writes=[tMISC]),
              dma("sp", CBF[:, :, :], d_cbf.ap(), writes=[tMISC]),
              dma("sp", CF[:, :, :], d_cf.ap(), writes=[tMISC]),
              dma("sp", KAUG[0:4, :], d_kaug.ap(), writes=[tMISC]),
              dma("sp", KAUG[64:68, :], d_kaug.ap(), writes=[tMISC])]
        for e in ("pe", "act", "dve"):
            SC.op(e, None, extra=cl)
        for l in range(L):
            lam_init = 0.8 - 0.6 * math.exp(-0.3 * l)
            t = nt()
            dma("sp", TMP[:, t, 0:256], bass.AP(d_lam, l * 256, [[0, 128], [1, 256]]), writes=[tTMP[t]])
            t2 = nt()
            for j in range(2):
                dve(lambda e, t=t, t2=t2, j=j: e.scalar_tensor_tensor(
                    TMP[:, t2, 0:64], TMP[:, t, 128 * j:128 * j + 64], 1.0, TMP[:, t, 128 * j + 64:128 * j + 128],
                    op0=ALU.mult, op1=ALU.mult, accum_out=SS[:, j:j + 1]), [tTMP[t]], [tTMP[t2], tSS])
            dve(lambda e: e.tensor_copy(SS[:, 15:16], SS[:, 14:15]), [tSS], [tSS])
            act(SS[:, 2:4], SS[:, 0:2], AF.Exp, [tSS], [tSS])
            sop = dve(lambda e: e.tensor_tensor(SS[:, 4:5], SS[:, 3:4], SS[:, 2:3], ALU.subtract), [tSS], [tSS])
            SC.op("dve", lambda e, l=l, lam_init=lam_init: e.tensor_scalar(NLAM[:, l:l + 1], SS[:, 4:5], -lam_init, None,
                                                                           op0=ALU.add), reads=[tSS], writes=[tMISC],
                  force=[sop])
            dve(lambda e, l=l, lam_init=lam_init: e.tensor_scalar(SUBL[:, l:l + 1], PAR[:, P_SUB + l:P_SUB + l + 1],
                                                                  1.0 - lam_init, None, op0=ALU.mult), [], [tMISC])

        for sq in range(nseq):
            wmode["m"] = "direct"
            mem_prep(sq)
            for l in range(L):
                for c in range(8):
                    dve(lambda e, l=l, c=c: e.memset(CARRY[:, l, c, :], 0.0), [], [tCARRY[l][c]])
            for ti in range(ntile):
                t0 = ti * TT
                wmode["m"] = "fill" if (sq == 0 and ti == 0) else "reuse"
                wmode["blk"] = 0
                dma("sp", H[:, :, :], xT.ap()[sq].rearrange("(k p) n -> p k n", p=128)[:, :, t0:t0 + TT], writes=tH)
                for l in range(L):
                    wmode["l"] = l
                    wmode["blk"] = 0
                    if "f" in SUBS:
                        ffn(l, w_f1i, w_f1o, 0)
                    if "m" in SUBS:
                        mixer(l, sq, ti)
                    if "x" in SUBS:
                        xattn(l, sq)
                    if "g" in SUBS:
                        ffn(l, w_f2i, w_f2o, 4)
                rmsnorm(lambda c: PAR[:, P_FN + c:P_FN + c + 1], final=True)
                out_dmas.append(dma("sp", yT.ap()[sq].rearrange("(k p) n -> p k n", p=128)[:, :, t0:t0 + TT],
                                    H[:, :, :], reads=tH))
        SC.op("sp", None, extra=out_dmas)

        SC.prepare()
        with nc.Block() as block:
            @block.tensor
            def _pe(e):
                SC.emit_engine("pe", e, sems, dsems)

            @block.scalar
            def _act(e):
                SC.emit_engine("act", e, sems, dsems)

            @block.vector
            def _dve(e):
                SC.emit_engine("dve", e, sems, dsems)

            @block.gpsimd
            def _pool(e):
                SC.emit_engine("pool", e, sems, dsems)

            @block.sync
            def _sp(e):
                SC.emit_engine("sp", e, sems, dsems)
    return nc


def _consts():
    bf = ml_dtypes.bfloat16
    cbf = np.zeros((128, 5, 128), np.float32)
    cbf[:, 0, :] = np.eye(128)
    cbf[:, 1, :] = 1.0 / 2048
    cbf[:, 2, :] = 1.0 / 128
    cbf[:, 3, :] = 1.0
    k = np.arange(128)[:, None]
    q = np.arange(128)[None, :]
    cbf[:, 4, :] = np.where(k > q, -30000.0, 0.0)
    cf = np.zeros((128, 2, 128), np.float32)
    cf[:, 0, :] = np.eye(128)
    cf[:, 1, :] = (k <= q).astype(np.float32)
    pos = np.arange(S)
    kaug = np.stack([256.0 * (pos // 256), (pos % 256).astype(np.float64), np.ones(S), np.ones(S)]).astype(np.float32)
    slopes = 2.0 ** (-8.0 * np.arange(1, 9) / 8)
    qaug = np.zeros((4, 8, S), np.float32)
    for h in range(8):
        qaug[0, h] = slopes[h]
        qaug[1, h] = slopes[h]
        qaug[2, h] = -slopes[h] * 256.0 * (pos // 256)
        qaug[3, h] = -slopes[h] * (pos % 256)
    return cbf.astype(bf), cf, kaug.astype(bf), qaug.astype(bf)


def _params(inp):
    P = np.zeros((128, P_TOT), np.float32)
    names = ["ffn1_norm", "mix_norm", "xattn_norm", "mem_norm", "ffn2_norm"]
    for l in range(DEPTH):
        for n, nm in enumerate(names):
            P[:, P_GN + l * 80 + n * 16: P_GN + l * 80 + (n + 1) * 16] = np.asarray(inp[nm][l]).reshape(16, 128).T
        for tap in range(3):
            P[:, P_CW + l * 24 + tap * 8: P_CW + l * 24 + (tap + 1) * 8] = np.asarray(inp["conv_w"][l, tap]).reshape(8, 128).T
        P[:, P_SUB + l] = np.asarray(inp["diff_subln"][l])
    P[:, P_FN:P_FN + 16] = np.asarray(inp["final_norm"]).reshape(16, 128).T
    P[:, P_EPS] = EPS
    return P


_NC_CACHE = {}


def run(inputs, ncores=8, nseq=2, ntile=4, nlayer=DEPTH):
    key = (nseq, ntile, nlayer)
    if key not in _NC_CACHE:
        _NC_CACHE[key] = build_program(nseq, ntile, nlayer)
    nc = _NC_CACHE[key]
    inp = {k: np.asarray(v) for k, v in inputs.items()}
    cbf, cf, kaug, qaug = _consts()
    shared = {
        "ffn1_w_in": inp["ffn1_w_in"], "ffn1_w_out": inp["ffn1_w_out"],
        "ffn2_w_in": inp["ffn2_w_in"], "ffn2_w_out": inp["ffn2_w_out"],
        "mix_w_in": inp["mix_w_in"], "diff_w_out": inp["diff_w_out"], "sgu_w_out": inp["sgu_w_out"],
        "conv_w_out": inp["conv_w_out"], "mix_w_o": inp["mix_w_o"], "xattn_w_q": inp["xattn_w_q"],
        "xattn_w_kv": inp["xattn_w_kv"], "xattn_w_o": inp["xattn_w_o"],
        "diff_lambda": np.ascontiguousarray(inp["diff_lambda"].reshape(DEPTH, 256)),
        "sgu_norm": inp["sgu_norm"], "sgu_b": np.ascontiguousarray(inp["sgu_b"].reshape(DEPTH, 1024)),
        "sgu_w_s": inp["sgu_w_s"], "params": _params(inp), "cbf": cbf, "cf32": cf, "kaug": kaug, "qaug": qaug,
    }
    in_maps = []
    for c in range(ncores):
        xs = inp["x"][c * nseq:(c + 1) * nseq]
        ms = inp["mem"][c * nseq:(c + 1) * nseq]
        m = dict(shared)
        m["xT"] = np.ascontiguousarray(xs.transpose(0, 2, 1))
        m["memT"] = np.ascontiguousarray(ms.transpose(0, 2, 1))
        in_maps.append(m)
    res = run_bass_kernel_spmd(nc, in_maps, core_ids=list(range(ncores)))
    outs = [np.asarray(r["yT"]).transpose(0, 2, 1) for r in res.results]
    if DEBUG:
        run.dbg = (np.asarray(res.results[0]["dbgf"]), np.asarray(res.results[0]["dbgb"]).astype(np.float32))
    return np.ascontiguousarray(np.concatenate(outs, axis=0)).astype(np.float32)


def kernel(**inputs):
    return run(inputs)
```
